# Optimizing a Trainium2 kernel written in Bass

```python
import math
import jax, jax.numpy as jnp
from jax import lax
import numpy as np

D_MODEL = 1024
BATCH = 8
SEQ = 4096
DEPTH = 1

HEAD_DIM = 64
N_SLOTS = 8
GROUP_PATTERNS = ((128, 1), (512, 4), (2048, 16))
N_ATTN_GROUPS = len(GROUP_PATTERNS)
ATTN_WIDTH = N_SLOTS * HEAD_DIM
QKV_WIDTH = N_ATTN_GROUPS * ATTN_WIDTH
BLOCK = 128
SSM_WIDTH = 512
SSM_GROUP = 16
SSM_GROUPS = SSM_WIDTH // SSM_GROUP
SSM_STATE = 64
DT_MIN = 1e-3
DT_MAX = 1e-1
N_BRANCHES = 2
D_IN = 3 * QKV_WIDTH + ATTN_WIDTH + 2 * SSM_WIDTH + N_BRANCHES * D_MODEL
DEEPNORM_ALPHA = (2.0 * DEPTH) ** 0.25
DEEPNORM_BETA = (8.0 * DEPTH) ** -0.25
LN_EPS = 1e-5

kernel_name = "hybrid_dilated_attn_s5_gated_deepnorm"


def _alibi_slopes(n):
    return 2.0 ** (-8.0 * jnp.arange(1, n + 1, dtype=jnp.float32) / n)


def _layer_norm(h, g, b):
    h = h.astype(jnp.float32)
    mu = h.mean(-1, keepdims=True)
    var = jnp.mean(jnp.square(h - mu), -1, keepdims=True)
    return (h - mu) * lax.rsqrt(var + LN_EPS) * g.astype(jnp.float32) + b.astype(jnp.float32)


def _dilated_group(q, k, v, window, dilation, slopes):
    b, L, h, dh = q.shape
    r = dilation
    span = r * BLOCK
    Lp = -(-L // span) * span
    n_sub = Lp // r
    nb = n_sub // BLOCK
    pad = ((0, 0), (0, Lp - L), (0, 0), (0, 0))

    def to_sub(t):
        t = jnp.pad(t, pad).reshape(b, n_sub, r, h, dh).transpose(0, 2, 3, 1, 4)
        return t.reshape(b, r, h, nb, BLOCK, dh)

    def band(t):
        prev = jnp.pad(t, ((0, 0), (0, 0), (0, 0), (1, 0), (0, 0), (0, 0)))[:, :, :, :-1]
        return jnp.concatenate([prev, t], axis=4)

    qs = to_sub(q)
    kb = band(to_sub(k))
    vb = band(to_sub(v))
    s = jnp.einsum('brhnqd,brhnkd->brhnqk', qs, kb,
                   preferred_element_type=jnp.float32) / math.sqrt(dh)
    qi = jnp.arange(BLOCK)[:, None]
    kj = jnp.arange(2 * BLOCK)[None, :]
    dist = BLOCK + qi - kj
    in_range = (jnp.arange(nb)[:, None, None] > 0) | (kj[None] >= BLOCK)
    valid = (dist >= 0) & (dist <= window // r) & in_range
    alibi = -slopes[:, None, None, None] * (dist * r).astype(jnp.float32)
    s = jnp.where(valid, s + alibi[None, None], -jnp.inf)
    m = s.max(-1)
    p = jnp.exp(s - m[..., None])
    l = p.sum(-1)
    o = jnp.einsum('brhnqk,brhnkd->brhnqd', p, vb.astype(jnp.float32)) / l[..., None]
    o = o.reshape(b, r, h, n_sub, dh).transpose(0, 3, 1, 2, 4).reshape(b, Lp, h, dh)[:, :L]
    m = m.reshape(b, r, h, n_sub).transpose(0, 3, 1, 2).reshape(b, Lp, h)[:, :L]
    l = l.reshape(b, r, h, n_sub).transpose(0, 3, 1, 2).reshape(b, Lp, h)[:, :L]
    return o, m, l


def _dilated_mixture(q, k, v):
    slopes = _alibi_slopes(N_SLOTS)
    outs, maxes, sums = [], [], []
    for g, (win, dil) in enumerate(GROUP_PATTERNS):
        o, m, l = _dilated_group(q[:, :, g], k[:, :, g], v[:, :, g], win, dil, slopes)
        outs.append(o)
        maxes.append(m)
        sums.append(l)
    m_all = jnp.stack(maxes)
    w = jnp.stack(sums) * jnp.exp(m_all - m_all.max(0, keepdims=True))
    return jnp.sum(w[..., None] * jnp.stack(outs), 0) / jnp.sum(w, 0)[..., None]


def _s5(u, lam_re, lam_im, log_dt, b_re, b_im, c_re, c_im, d_skip):
    f32 = jnp.float32
    bsz, L, _ = u.shape
    uf = u.astype(f32)
    ug = uf.reshape(bsz, L, SSM_GROUPS, SSM_GROUP)
    lr, li = lam_re.astype(f32), lam_im.astype(f32)
    dt = jnp.exp(log_dt.astype(f32))[:, None]
    mag = jnp.exp(lr * dt)
    ab_re, ab_im = mag * jnp.cos(li * dt), mag * jnp.sin(li * dt)
    nr, ni = ab_re - 1.0, ab_im
    den = lr * lr + li * li
    coef_re = (nr * lr + ni * li) / den
    coef_im = (ni * lr - nr * li) / den
    br, bi = b_re.astype(f32), b_im.astype(f32)
    bb_re = coef_re[..., None] * br - coef_im[..., None] * bi
    bb_im = coef_re[..., None] * bi + coef_im[..., None] * br
    bu_re = jnp.einsum('gpc,blgc->blgp', bb_re, ug)
    bu_im = jnp.einsum('gpc,blgc->blgp', bb_im, ug)
    a_re = jnp.broadcast_to(ab_re, bu_re.shape)
    a_im = jnp.broadcast_to(ab_im, bu_im.shape)

    def combine(e1, e2):
        a1r, a1i, b1r, b1i = e1
        a2r, a2i, b2r, b2i = e2
        return (a2r * a1r - a2i * a1i,
                a2r * a1i + a2i * a1r,
                a2r * b1r - a2i * b1i + b2r,
                a2r * b1i + a2i * b1r + b2i)

    _, _, xr, xi = lax.associative_scan(combine, (a_re, a_im, bu_re, bu_im), axis=1)
    y = (jnp.einsum('gcp,blgp->blgc', c_re.astype(f32), xr)
         - jnp.einsum('gcp,blgp->blgc', c_im.astype(f32), xi))
    return y.reshape(bsz, L, SSM_WIDTH) + d_skip.astype(f32) * uf


def setup_inputs(seed: int = 0) -> dict:
    key = jax.random.key(seed)
    ks = jax.random.split(key, 20)
    f32 = jnp.float32

    def nrm(k, shape, scale):
        return jax.random.normal(k, shape, f32) * scale

    x = jax.random.normal(ks[0], (BATCH, SEQ, D_MODEL), f32)
    w_in = nrm(ks[1], (DEPTH, D_MODEL, D_IN), D_MODEL ** -0.5)
    w_in = w_in.at[:, :, 2 * QKV_WIDTH:3 * QKV_WIDTH].multiply(DEEPNORM_BETA)
    lam_re = -0.5 + nrm(ks[2], (DEPTH, SSM_GROUPS, SSM_STATE), 0.01)
    lam_im = (math.pi * jnp.arange(SSM_STATE, dtype=f32))[None, None, :] + nrm(ks[3], (DEPTH, SSM_GROUPS, SSM_STATE), 0.01)
    log_dt = jax.random.uniform(ks[4], (DEPTH, SSM_GROUPS), f32, math.log(DT_MIN), math.log(DT_MAX))
    b_re = nrm(ks[5], (DEPTH, SSM_GROUPS, SSM_STATE, SSM_GROUP), (2.0 * SSM_GROUP) ** -0.5)
    b_im = nrm(ks[6], (DEPTH, SSM_GROUPS, SSM_STATE, SSM_GROUP), (2.0 * SSM_GROUP) ** -0.5)
    c_re = nrm(ks[7], (DEPTH, SSM_GROUPS, SSM_GROUP, SSM_STATE), (2.0 * SSM_STATE) ** -0.5)
    c_im = nrm(ks[8], (DEPTH, SSM_GROUPS, SSM_GROUP, SSM_STATE), (2.0 * SSM_STATE) ** -0.5)
    d_skip = nrm(ks[9], (DEPTH, SSM_WIDTH), 1.0)
    w_glu = nrm(ks[10], (DEPTH, SSM_WIDTH, SSM_WIDTH), SSM_WIDTH ** -0.5)
    b_glu = nrm(ks[11], (DEPTH, SSM_WIDTH), 0.01)
    w_attn_up = nrm(ks[12], (DEPTH, ATTN_WIDTH, D_MODEL), ATTN_WIDTH ** -0.5)
    w_ssm_up = nrm(ks[13], (DEPTH, SSM_WIDTH, D_MODEL), SSM_WIDTH ** -0.5)
    w_o = nrm(ks[14], (DEPTH, D_MODEL, D_MODEL), DEEPNORM_BETA * D_MODEL ** -0.5)
    ln_g = 1.0 + nrm(ks[15], (DEPTH, D_MODEL), 0.01)
    ln_b = nrm(ks[16], (DEPTH, D_MODEL), 0.01)
    return {"x": x, "w_in": w_in, "lam_re": lam_re, "lam_im": lam_im, "log_dt": log_dt,
            "b_re": b_re, "b_im": b_im, "c_re": c_re, "c_im": c_im, "d_skip": d_skip,
            "w_glu": w_glu, "b_glu": b_glu, "w_attn_up": w_attn_up, "w_ssm_up": w_ssm_up,
            "w_o": w_o, "ln_g": ln_g, "ln_b": ln_b}


def reference(x, w_in, lam_re, lam_im, log_dt, b_re, b_im, c_re, c_im, d_skip,
              w_glu, b_glu, w_attn_up, w_ssm_up, w_o, ln_g, ln_b):
    bsz, L, _ = x.shape
    h = x
    for layer in range(DEPTH):
        proj = jnp.einsum('bld,de->ble', h, w_in[layer])
        o0 = 0
        q = proj[..., o0:o0 + QKV_WIDTH]; o0 += QKV_WIDTH
        k = proj[..., o0:o0 + QKV_WIDTH]; o0 += QKV_WIDTH
        v = proj[..., o0:o0 + QKV_WIDTH]; o0 += QKV_WIDTH
        z_a = proj[..., o0:o0 + ATTN_WIDTH]; o0 += ATTN_WIDTH
        u_s = proj[..., o0:o0 + SSM_WIDTH]; o0 += SSM_WIDTH
        z_s = proj[..., o0:o0 + SSM_WIDTH]; o0 += SSM_WIDTH
        gate_logits = proj[..., o0:o0 + N_BRANCHES * D_MODEL].astype(jnp.float32)
        hs = (bsz, L, N_ATTN_GROUPS, N_SLOTS, HEAD_DIM)
        attn = _dilated_mixture(q.reshape(hs), k.reshape(hs), v.reshape(hs)).reshape(bsz, L, ATTN_WIDTH)
        attn = attn * jax.nn.silu(z_a.astype(jnp.float32))
        y_a = jnp.einsum('blc,cd->bld', attn, w_attn_up[layer])
        y = jax.nn.gelu(_s5(u_s, lam_re[layer], lam_im[layer], log_dt[layer], b_re[layer], b_im[layer],
                            c_re[layer], c_im[layer], d_skip[layer]))
        y = y * jax.nn.sigmoid(jnp.einsum('blc,ce->ble', y, w_glu[layer]) + b_glu[layer])
        y = y * jax.nn.silu(z_s.astype(jnp.float32))
        y_s = jnp.einsum('blc,cd->bld', y, w_ssm_up[layer])
        g_a = jax.nn.sigmoid(gate_logits[..., :D_MODEL])
        g_s = jax.nn.sigmoid(gate_logits[..., D_MODEL:])
        merged = g_a * y_a + g_s * y_s
        out = jnp.einsum('bld,de->ble', merged, w_o[layer])
        h = _layer_norm(DEEPNORM_ALPHA * h.astype(jnp.float32) + out, ln_g[layer], ln_b[layer]).astype(x.dtype)
    return h
```

```python
import math
import os
STAGES = os.environ.get('KSTAGES', 'XSAM')
CUT = int(os.environ.get('KCUT', '99'))
STRICT = os.environ.get('KSTRICT', '1') == '1'
from contextlib import ExitStack

import numpy as np
import concourse.bass as bass
import concourse.mybir as mybir
from concourse.bass_utils import run_bass_kernel_spmd

F32 = mybir.dt.float32
BF16 = mybir.dt.bfloat16
AF = mybir.ActivationFunctionType
ALU = mybir.AluOpType

L = 4096
D = 1024
DIL = (1, 4, 16)
ALPHA = 2.0 ** 0.25
EPS = 1e-5


class _Op:
    __slots__ = ("eng", "fn", "reads", "writes", "dma_key", "deps", "inc", "cnt", "dma_cnt")

    def __init__(self, eng, fn, reads, writes, dma_key):
        self.eng = eng
        self.fn = fn
        self.reads = tuple(reads)
        self.writes = tuple(writes)
        self.dma_key = dma_key
        self.deps = ()
        self.inc = False
        self.cnt = 0
        self.dma_cnt = 0


class Prog:
    ENGS = ("pe", "act", "dve", "pool", "sp")

    def __init__(self, nc):
        self.nc = nc
        self.ops = []

    def add(self, eng, fn, reads=(), writes=(), dma_key=None):
        self.ops.append(_Op(eng, fn, reads, writes, dma_key))

    def pe(self, fn, reads=(), writes=()):
        self.add("pe", fn, reads, writes)

    def act(self, fn, reads=(), writes=()):
        self.add("act", fn, reads, writes)

    def dve(self, fn, reads=(), writes=()):
        self.add("dve", fn, reads, writes)

    def pool(self, fn, reads=(), writes=()):
        self.add("pool", fn, reads, writes)

    def dma(self, q, out, in_, key, reads=(), writes=(), **kw):
        self.add(q, lambda e: e.dma_start(out=out, in_=in_, **kw), reads, writes, dma_key=key)

    def alias(self, new, old):
        self.ops.append(("alias", tuple(new), tuple(old)))

    def analyze(self):
        last_w = {}
        readers = {}
        real = []
        for item in self.ops:
            if isinstance(item, tuple):
                _, new, old = item
                users = set()
                for o in old:
                    if last_w.get(o) is not None:
                        users.add(last_w[o])
                    users.update(readers.get(o, ()))
                for r in new:
                    last_w[r] = None
                    readers[r] = list(users)
                continue
            op = item
            i = len(real)
            real.append(op)
            deps = set()
            for r in op.reads:
                w = last_w.get(r)
                if w is not None:
                    deps.add(w)
                if isinstance(r, tuple) and r[0] == "ps":
                    for j in readers.get(r, ()):
                        if real[j].eng != op.eng:
                            deps.add(j)
            for r in op.writes:
                w = last_w.get(r)
                if w is not None:
                    deps.add(w)
                deps.update(readers.get(r, ()))
            keep = []
            for j in deps:
                if j == i:
                    continue
                oj = real[j]
                if oj.dma_key is None and op.dma_key is None and oj.eng == op.eng:
                    if op.eng == "pe":
                        continue
                    if not STRICT and not any(r in oj.writes for r in op.reads):
                        continue
                keep.append(j)
            op.deps = tuple(keep)
            for r in op.reads:
                readers.setdefault(r, []).append(i)
            for r in op.writes:
                last_w[r] = i
                readers[r] = []
        self.real = real
        for op in real:
            for j in op.deps:
                if real[j].dma_key is None:
                    real[j].inc = True
        cnt = {e: 0 for e in self.ENGS}
        dcnt = {}
        for op in real:
            if op.dma_key is not None:
                dcnt[op.dma_key] = dcnt.get(op.dma_key, 0) + 1
                op.dma_cnt = dcnt[op.dma_key]
            elif op.inc:
                cnt[op.eng] += 1
                op.cnt = cnt[op.eng]
        self.dma_keys = list(dcnt.keys())

    def emit(self):
        nc = self.nc
        self.analyze()
        ops = self.real
        with ExitStack() as st:
            esem = {e: st.enter_context(nc.semaphore("s_" + e)) for e in self.ENGS}
            dsem = {}
            for k in self.dma_keys:
                dsem[k] = st.enter_context(nc.semaphore("d_%d" % len(dsem)))
            block = st.enter_context(nc.Block())

            def run(eng_name, eng):
                waited = {}
                for op in ops:
                    if op.eng != eng_name:
                        continue
                    need = {}
                    for j in op.deps:
                        oj = ops[j]
                        if oj.dma_key is not None:
                            sk, val = ("d", oj.dma_key), 16 * oj.dma_cnt
                        else:
                            sk, val = ("e", oj.eng), oj.cnt
                        if val > need.get(sk, 0):
                            need[sk] = val
                    for sk, val in need.items():
                        if waited.get(sk, 0) >= val:
                            continue
                        waited[sk] = val
                        sem = dsem[sk[1]] if sk[0] == "d" else esem[sk[1]]
                        eng.wait_ge(sem, val)
                    ins = op.fn(eng) if op.fn is not None else None
                    if op.dma_key is not None:
                        ins.then_inc(dsem[op.dma_key], 16)
                    elif op.inc:
                        ins.then_inc(esem[eng_name], 1)

            @block.tensor
            def _(e):
                run("pe", e)

            @block.scalar
            def _(e):
                run("act", e)

            @block.vector
            def _(e):
                run("dve", e)

            @block.gpsimd
            def _(e):
                run("pool", e)

            @block.sync
            def _(e):
                run("sp", e)


def build(dbg=False):
    nc = bass.Bass("TRN2", target_bir_lowering=False)

    def din(name, shape, dt=F32):
        return nc.dram_tensor(name, list(shape), dt, kind="ExternalInput").ap()

    x = din("x", [L, D])
    w_in = din("w_in", [D, 8192])
    w_glu_d = din("w_glu", [512, 512])
    w_au_d = din("w_attn_up", [512, D])
    w_su_d = din("w_ssm_up", [512, D])
    w_o_d = din("w_o", [D, D])
    pB_lr = din("pB_lr", [128, 256])
    pB_li = din("pB_li", [128, 256])
    pB_ldt = din("pB_ldt", [128, 256])
    pB_br = din("pB_br", [128, 256])
    pB_bi = din("pB_bi", [128, 256])
    pC_c = din("pC_c", [128, 512])
    pS_lr = din("pS_lr", [128, 16])
    pS_li = din("pS_li", [128, 16])
    pS_ldt = din("pS_ldt", [128, 16])
    dsk_d = din("dsk", [128, 4])
    bglu_d = din("bglu", [128, 4])
    lng_d = din("lng", [128, D])
    lnb_d = din("lnb", [128, D])
    ident_d = din("c_ident", [128, 128])
    rowmask_d = din("c_rowmask", [128, 8])
    mask_d = din("c_mask", [24, 128, 256])
    out_d = nc.dram_tensor("out", [L, D], F32, kind="ExternalOutput").ap()
    ys_d = nc.dram_tensor("ys_scr", [128, 4, L], BF16, kind="ExternalOutput" if dbg else "Internal").ap()
    if dbg:
        attn_dbg = nc.dram_tensor("attn_dbg", [128, 4, L], BF16, kind="ExternalOutput").ap()

    P = Prog(nc)
    with ExitStack() as st:
        def sb(name, shape, dt):
            return st.enter_context(nc.sbuf_tensor(name, list(shape), dt))

        pbig = st.enter_context(nc.psum_tensor("pbig", [128, 4096], F32))[:, :]
        pb = [pbig[:, 512 * i:512 * (i + 1)] for i in range(8)]
        PS = lambda i: ("ps", i)

        xT = sb("xT", [128, 8, L], BF16)
        attnT = sb("attnT", [128, 4, L], BF16)
        ident = sb("ident", [128, 128], F32)
        rowmask = sb("rowmask", [128, 8], F32)
        dsk = sb("dskt", [128, 4], F32)
        bglu = sb("bglut", [128, 4], F32)
        ARENA_B = 108 * 1024
        arena = sb("arena", [128, ARENA_B // 2], BF16)

        class Carver:
            def __init__(self):
                self.off = 0
                self.names = []

            def get(self, name, shape, dt):
                n = int(np.prod(shape[1:]))
                nbytes = n * (4 if dt == F32 else 2)
                nbytes = (nbytes + 63) // 64 * 64
                a = arena[:, self.off // 2:(self.off + nbytes) // 2]
                self.off += nbytes
                assert self.off <= ARENA_B, (name, self.off)
                if dt == F32:
                    a = a.bitcast(F32)
                a = a[:, 0:n]
                if len(shape) == 3:
                    a = a.rearrange("p (a b) -> p a b", a=shape[1])
                elif len(shape) == 4:
                    a = a.rearrange("p (a b c) -> p a b c", a=shape[1], b=shape[2])
                self.names.append(name)
                return a

        P.dma("sp", ident[:], ident_d, "ident", writes=["ident"])
        P.dma("sp", rowmask[:], rowmask_d, "rowmask", writes=["rowmask"])
        P.dma("sp", dsk[:], dsk_d, "dsk", writes=["dsk"])
        P.dma("sp", bglu[:], bglu_d, "bglu", writes=["bglu"])

        NXS = 4
        xs = [attnT[:, i, 0:2 * D].bitcast(F32) for i in range(NXS)]
        for tt in range(32):
            sl = tt % NXS
            P.dma("sp", xs[sl], x[tt * 128:(tt + 1) * 128, :], ("xs", sl), writes=[("xs", sl)])
            for h in range(2):
                bank = (2 * tt + h) % 4

                def ftr(e, sl=sl, h=h, bank=bank):
                    ins = None
                    for j in range(4):
                        dt_ = 4 * h + j
                        ins = e.transpose(out=pb[bank][:, j * 128:(j + 1) * 128],
                                          in_=xs[sl][:, dt_ * 128:(dt_ + 1) * 128], identity=ident[:])
                    return ins

                P.pe(ftr, reads=[("xs", sl), "ident"], writes=[PS(bank)])
                P.act(lambda e, h=h, bank=bank, tt=tt: e.copy(
                    out=xT[:, 4 * h:4 * h + 4, tt * 128:(tt + 1) * 128],
                    in_=pb[bank][:, :].rearrange("p (a b) -> p a b", a=4)),
                    reads=[PS(bank)], writes=[("xT", tt)])
        XT_ALL = [("xT", tt) for tt in range(32)]

        def load_w(dst, src_ap, key, res):
            P.dma("pool", dst, src_ap.rearrange("(k p) c -> p k c", p=128), key, writes=[res])

        CS = Carver()
        w_u = CS.get("w_u", [128, 8, 512], BF16)
        w_zs = CS.get("w_zs", [128, 8, 512], BF16)
        w_glu = CS.get("w_glu", [128, 4, 512], BF16)
        BbT = CS.get("BbT", [128, 4, 128], BF16)
        BbTpad = CS.get("BbTpad", [128, 32, 128], BF16)
        CmS = CS.get("CmS", [128, 8, 2, 48], BF16)
        TMr = CS.get("TMr", [128, 16, 128], BF16)
        TMi = CS.get("TMi", [128, 16, 128], BF16)
        TDr = CS.get("TDr", [128, 16, 128], BF16)
        TDi = CS.get("TDi", [128, 16, 128], BF16)
        A1c = CS.get("A1c", [128, 16, 2, 2], F32)
        ALc = CS.get("ALc", [128, 16, 2, 2], F32)
        ONES = CS.get("ONES", [128, 1024], BF16)
        carry = CS.get("carry", [128, 16, 2], F32)
        P4 = CS.get("P4", [128, 16, 2, 2], F32)
        seq_off = CS.off
        SEQ = CS.get("SEQ", [128, 16, 2, 128], F32)
        W16 = CS.get("W16", [128, 16, 2, 128], BF16)
        pw_off = CS.off
        Xb = CS.get("Xb", [128, 16, 2, 128], BF16)
        T1m = CS.get("T1m", [128, 4, 2, 128], BF16)
        T2m = CS.get("T2m", [128, 4, 2, 128], BF16)
        T1d = CS.get("T1d", [128, 4, 2, 128], BF16)
        T2d = CS.get("T2d", [128, 4, 2, 128], BF16)
        ts_off = CS.off
        tS = CS.get("tS", [128, 512], F32)
        sqS = CS.get("sqS", [128, 512], BF16)
        inS = CS.get("inS", [128, 512], BF16)
        sgS = CS.get("sgS", [128, 512], BF16)
        uT = [CS.get("uT%d" % i, [128, 4, 128], BF16) for i in range(2)]
        dsu = [CS.get("dsu%d" % i, [128, 4, 128], BF16) for i in range(2)]
        tab_end = CS.off
        yT = CS.get("yT", [128, 4, 128], BF16)
        s1 = CS.get("s1", [128, 4, 128], BF16)
        s2 = CS.get("s2", [128, 512], BF16)
        m1 = CS.get("m1", [128, 512], BF16)
        m2 = CS.get("m2", [128, 512], BF16)
        ysc = [CS.get("ysc%d" % i, [128, 4, 128], BF16) for i in range(2)]
        pct_off = CS.off
        pCt = CS.get("pCt", [128, 512], F32)
        CS5 = Carver()
        CS5.off = pct_off
        Bq16 = CS5.get("Bq16", [128, 4, 2, 128], BF16)
        NPS = 24
        CS2 = Carver()
        CS2.off = seq_off
        pscr = [CS2.get("pscr%d" % i, [128, 256], F32) for i in range(NPS)]
        assert CS2.off <= pw_off
        CS3 = Carver()
        CS3.off = pw_off
        PWr = CS3.get("PWr", [128, 16, 128], F32)
        PWi = CS3.get("PWi", [128, 16, 128], F32)
        assert CS3.off <= ts_off
        CS4 = Carver()
        CS4.off = ts_off
        tA = CS4.get("tA", [128, 16, 64], F32)
        tB = CS4.get("tB", [128, 16, 64], F32)
        assert CS4.off <= tab_end
        XS_NAMES = [("xs", i) for i in range(NXS)]
        S_ALL = (["w_u", "w_zs", "w_glu", "BbT", "BbTpad", "CmS", "TMr", "TMi", "TDr", "TDi", "A1c", "ALc", "ONES", "carry", "P4",
                  "pCt", "PWr", "PWi", "tA", "tB"] + [("pscr", i) for i in range(NPS)])

        load_w(w_u, w_in[:, 5120:5632], "w_u", "w_u")
        load_w(w_zs, w_in[:, 5632:6144], "w_zs", "w_zs")
        load_w(w_glu, w_glu_d, "w_glu", "w_glu")

        scr_i = [0]

        def new_scr(n):
            i = scr_i[0]
            scr_i[0] += 1
            assert i < NPS
            return pscr[i][:, 0:n], ("pscr", i)

        def tt(out, a, b, op):
            P.dve(lambda e: e.tensor_tensor(out=out[0], in0=a[0], in1=b[0], op=op),
                  reads=[a[1], b[1]], writes=[out[1]])

        def ts(out, a, s1_, s2_, op0, op1=None):
            if op1 is None:
                P.dve(lambda e: e.tensor_scalar(out=out[0], in0=a[0], scalar1=s1_, scalar2=None, op0=op0),
                      reads=[a[1]], writes=[out[1]])
            else:
                P.dve(lambda e: e.tensor_scalar(out=out[0], in0=a[0], scalar1=s1_, scalar2=s2_, op0=op0, op1=op1),
                      reads=[a[1]], writes=[out[1]])

        def exp_poly(xin, n, nterms, nsq):
            y = new_scr(n)
            ts(y, xin, 1.0 / (1 << nsq), None, ALU.mult)
            p = new_scr(n)
            t = new_scr(n)
            ts(p, y, 1.0 / nterms, 1.0, ALU.mult, ALU.add)
            for k in range(nterms - 1, 0, -1):
                tt(t, p, y, ALU.mult)
                ts(p, t, 1.0 / k, 1.0, ALU.mult, ALU.add)
            for _ in range(nsq):
                tt(t, p, p, ALU.mult)
                p, t = t, p
            return p

        hpi = sb("hpi", [128, 1], F32)
        P.pool(lambda e: e.memset(hpi[:], math.pi / 2), writes=["hpi"])

        def cplx_a(lr, li, ldt, n, want_coef, want_inv):
            dt = exp_poly(ldt, n, 12, 3)
            xr = new_scr(n)
            th = new_scr(n)
            tt(xr, lr, dt, ALU.mult)
            tt(th, li, dt, ALU.mult)
            mag = exp_poly(xr, n, 7, 0)
            s = new_scr(n)
            c = new_scr(n)
            P.act(lambda e: e.activation(out=s[0], in_=th[0], func=AF.Sin, scale=0.125), reads=[th[1]], writes=[s[1]])
            P.act(lambda e: e.activation(out=c[0], in_=th[0], func=AF.Sin, scale=-0.125, bias=hpi[:, 0:1]),
                  reads=[th[1], "hpi"], writes=[c[1]])
            t1 = new_scr(n)
            t2 = new_scr(n)
            for _ in range(3):
                tt(t1, s, c, ALU.mult)
                tt(t2, s, s, ALU.mult)
                ts(s, t1, 2.0, None, ALU.mult)
                ts(c, t2, -2.0, 1.0, ALU.mult, ALU.add)
            ar = new_scr(n)
            ai = new_scr(n)
            tt(ar, mag, c, ALU.mult)
            tt(ai, mag, s, ALU.mult)
            res = {"ar": ar, "ai": ai}
            if want_inv:
                ts(t1, xr, -1.0, None, ALU.mult)
                minv = exp_poly(t1, n, 7, 0)
                ir = new_scr(n)
                ii = new_scr(n)
                tt(ir, minv, c, ALU.mult)
                tt(ii, minv, s, ALU.mult)
                ts(ii, ii, -1.0, None, ALU.mult)
                res["ir"], res["ii"] = ir, ii
            if want_coef:
                nr = t1
                ts(nr, ar, -1.0, None, ALU.add)
                den = t2
                tt(den, lr, lr, ALU.mult)
                tt(th, li, li, ALU.mult)
                tt(den, den, th, ALU.add)
                P.dve(lambda e: e.reciprocal(out=den[0], in_=den[0]), reads=[den[1]], writes=[den[1]])
                cr = new_scr(n)
                ci = new_scr(n)
                tt(cr, nr, lr, ALU.mult)
                tt(th, ai, li, ALU.mult)
                tt(cr, cr, th, ALU.add)
                tt(cr, cr, den, ALU.mult)
                tt(ci, ai, lr, ALU.mult)
                tt(th, nr, li, ALU.mult)
                tt(ci, ci, th, ALU.subtract)
                tt(ci, ci, den, ALU.mult)
                res["cr"], res["ci"] = cr, ci
            return res

        def load_scr(src, n):
            t = new_scr(n)
            P.dma("sp", t[0], src, t[1], writes=[t[1]])
            return t

        lrB = load_scr(pB_lr, 256)
        liB = load_scr(pB_li, 256)
        ldtB = load_scr(pB_ldt, 256)
        brB = load_scr(pB_br, 256)
        biB = load_scr(pB_bi, 256)
        rB = cplx_a(lrB, liB, ldtB, 256, True, False)
        crB, ciB = rB["cr"], rB["ci"]
        t1 = lrB
        t2 = liB
        BbT_re = (BbT[:, :, 0:64], "BbT")
        BbT_im = (BbT[:, :, 64:128], "BbT")
        v3 = lambda a: (a[0].rearrange("p (k q) -> p k q", k=4), a[1])
        tt(t1, crB, brB, ALU.mult)
        tt(t2, ciB, biB, ALU.mult)
        tt(BbT_re, v3(t1), v3(t2), ALU.subtract)
        tt(t1, crB, biB, ALU.mult)
        tt(t2, ciB, brB, ALU.mult)
        tt(BbT_im, v3(t1), v3(t2), ALU.add)
        for gl in range(8):
            P.dve(lambda e, gl=gl: e.tensor_scalar(out=BbTpad[:, gl:32:8, :], in0=BbT[:, :, :], scalar1=rowmask[:, gl:gl + 1],
                                                   scalar2=None, op0=ALU.mult), reads=["BbT", "rowmask"], writes=["BbTpad"])
        scr_i[0] = 0
        lrS = load_scr(pS_lr, 16)
        liS = load_scr(pS_li, 16)
        ldtS = load_scr(pS_ldt, 16)
        rS = cplx_a(lrS, liS, ldtS, 16, False, True)

        def cat_table(dst, dname, re, im):
            P.dve(lambda e: e.tensor_copy(out=dst[:, :, 0, 0], in_=re[0]), reads=[re[1]], writes=[dname])
            P.dve(lambda e: e.tensor_copy(out=dst[:, :, 0, 1], in_=im[0]), reads=[im[1]], writes=[dname])
            P.dve(lambda e: e.tensor_scalar(out=dst[:, :, 1, 0], in0=im[0], scalar1=-1.0, scalar2=None, op0=ALU.mult),
                  reads=[im[1]], writes=[dname])
            P.dve(lambda e: e.tensor_copy(out=dst[:, :, 1, 1], in_=re[0]), reads=[re[1]], writes=[dname])

        cat_table(A1c, "A1c", rS["ar"], rS["ai"])

        def power_table(br, bi, dr, di, last=None):
            P.dve(lambda e: e.memset(PWr[:, :, 0:1], 1.0), writes=["PWr"])
            P.dve(lambda e: e.memset(PWi[:, :, 0:1], 0.0), writes=["PWi"])
            P.dve(lambda e: e.tensor_copy(out=PWr[:, :, 1], in_=br[0]), reads=[br[1]], writes=["PWr"])
            P.dve(lambda e: e.tensor_copy(out=PWi[:, :, 1], in_=bi[0]), reads=[bi[1]], writes=["PWi"])
            k = 1
            while k < 128:
                n = min(k, 127 - k)
                if n > 0:
                    sr, si = PWr[:, :, 1:1 + n], PWi[:, :, 1:1 + n]
                    orr, oi = PWr[:, :, k + 1:k + 1 + n], PWi[:, :, k + 1:k + 1 + n]
                    cr = PWr[:, :, k:k + 1].to_broadcast([128, 16, n])
                    ci = PWi[:, :, k:k + 1].to_broadcast([128, 16, n])
                    a_, b_ = tA[:, :, 0:n], tB[:, :, 0:n]
                    rw = ["PWr", "PWi"]
                    P.dve(lambda e, sr=sr, cr=cr, a_=a_: e.tensor_tensor(out=a_, in0=sr, in1=cr, op=ALU.mult), reads=rw, writes=["tA"])
                    P.dve(lambda e, si=si, ci=ci, b_=b_: e.tensor_tensor(out=b_, in0=si, in1=ci, op=ALU.mult), reads=rw, writes=["tB"])
                    P.dve(lambda e, orr=orr, a_=a_, b_=b_: e.tensor_tensor(out=orr, in0=a_, in1=b_, op=ALU.subtract),
                          reads=["tA", "tB", "PWr"], writes=["PWr"])
                    P.dve(lambda e, sr=sr, ci=ci, a_=a_: e.tensor_tensor(out=a_, in0=sr, in1=ci, op=ALU.mult), reads=rw, writes=["tA"])
                    P.dve(lambda e, si=si, cr=cr, b_=b_: e.tensor_tensor(out=b_, in0=si, in1=cr, op=ALU.mult), reads=rw, writes=["tB"])
                    P.dve(lambda e, oi=oi, a_=a_, b_=b_: e.tensor_tensor(out=oi, in0=a_, in1=b_, op=ALU.add),
                          reads=["tA", "tB", "PWi"], writes=["PWi"])
                k *= 2
            if last is not None:
                cat_table(last, "ALc", (PWr[:, :, 127], "PWr"), (PWi[:, :, 127], "PWi"))
            P.act(lambda e: e.copy(out=dr, in_=PWr), reads=["PWr"], writes=[dname_of[id(dr)]])
            P.act(lambda e: e.copy(out=di, in_=PWi), reads=["PWi"], writes=[dname_of[id(di)]])

        dname_of = {id(TMr): "TMr", id(TMi): "TMi", id(TDr): "TDr", id(TDi): "TDi"}
        power_table(rS["ir"], rS["ii"], TMr, TMi)
        power_table(rS["ar"], rS["ai"], TDr, TDi, last=ALc)
        P.dma("sp", pCt, pC_c, "pCt", writes=["pCt"])
        P.pool(lambda e: e.memset(CmS, 0.0), writes=["CmS"])
        pC4 = pCt.rearrange("p (g r c) -> p g r c", g=16, r=2)
        for ri, sgn in ((0, 1.0), (1, -1.0)):
            for par, c0 in ((0, 0), (1, 32)):
                P.dve(lambda e, ri=ri, sgn=sgn, par=par, c0=c0: e.tensor_scalar(
                    out=CmS[:, :, ri, c0:c0 + 16], in0=pC4[:, par:16:2, ri, :], scalar1=sgn, scalar2=None, op0=ALU.mult),
                    reads=["pCt", "CmS"], writes=["CmS"])
        P.pool(lambda e: e.memset(carry, 0.0), writes=["carry"])
        P.pool(lambda e: e.memset(ONES, 1.0), writes=["ONES"])
        P.pool(lambda e: e.memset(ONES[:, 0:1024:128], 0.0), reads=["ONES"], writes=["ONES"])
        SEG_NAMES = ["T1m", "T2m", "T1d", "T2d", "tS", "sqS", "inS", "sgS"] + [(n, q) for n in ("SEQ", "W16", "Xb") for q in range(4)] + [(n, i) for n in ("uT", "dsu") for i in range(2)]
        P.alias(SEG_NAMES, [("pscr", i) for i in range(NPS)] + ["PWr", "PWi", "tA", "tB"])
        P.alias(["Bq16"], ["pCt"])

        T = 128
        NSEG = L // T if 'S' in STAGES else 0

        def seg_F1(sg):
            t0 = sg * T
            db = sg % 2
            xres_seg = [("xT", sg)]

            def fu(e, t0=t0):
                ins = None
                for k in range(4):
                    for d_ in range(8):
                        ins = e.matmul(pb[0][:, k * 128:(k + 1) * 128], lhsT=w_u[:, d_, k * 128:(k + 1) * 128],
                                       rhs=xT[:, d_, t0:t0 + T], start=(d_ == 0), stop=(d_ == 7))
                return ins
            P.pe(fu, reads=["w_u"] + xres_seg, writes=[PS(0)])
            p0v = pb[0][:, :].rearrange("p (a b) -> p a b", a=4)
            P.act(lambda e, db=db: e.copy(out=uT[db], in_=p0v), reads=[PS(0)], writes=[("uT", db)])
            for k in range(4):
                P.act(lambda e, db=db, k=k: e.activation(out=dsu[db][:, k, :], in_=pb[0][:, k * 128:(k + 1) * 128], func=AF.Copy, scale=dsk[:, k:k + 1]),
                      reads=[PS(0), "dsk"], writes=[("dsu", db)])
            for qd in range(4):
                qb = 1 + 2 * (qd % 2)
                Q = pbig[:, 512 * qb:512 * qb + 1024]
                Q4 = Q.rearrange("p (a b c) -> p a b c", a=4, b=2)

                def fbu(e, qd=qd, Q=Q, db=db):
                    ins = None
                    for gi in range(4):
                        for gh in range(2):
                            g = 16 * gh + 4 * qd + gi
                            k = g // 8
                            for ri in range(2):
                                c0 = (gi * 2 + ri) * 128
                                ins = e.matmul(Q[64 * gh:64 * gh + 64, c0:c0 + 128], lhsT=BbTpad[:, g, 64 * ri:64 * ri + 64],
                                               rhs=uT[db][:, k, :], start=True, stop=True)
                    return ins
                P.pe(fbu, reads=["BbTpad", ("uT", db)], writes=[PS(qb), PS(qb + 1)])
                g4 = slice(4 * qd, 4 * qd + 4)
                tmr = TMr[:, g4, :].unsqueeze(2).to_broadcast([128, 4, 2, 128])
                tmi = TMi[:, g4, :].unsqueeze(2).to_broadcast([128, 4, 2, 128])
                rdq = [PS(qb), PS(qb + 1)]
                P.act(lambda e, Q4=Q4: e.copy(out=Bq16, in_=Q4), reads=rdq, writes=["Bq16"])
                P.dve(lambda e, tmr=tmr: e.tensor_tensor(out=T1m, in0=Bq16, in1=tmr, op=ALU.mult), reads=["Bq16", "TMr"], writes=["T1m"])
                P.dve(lambda e, tmi=tmi: e.tensor_tensor(out=T2m, in0=Bq16, in1=tmi, op=ALU.mult), reads=["Bq16", "TMi"], writes=["T2m"])
                P.dve(lambda e, g4=g4: e.tensor_tensor(out=SEQ[:, g4, 0, :], in0=T1m[:, :, 0, :], in1=T2m[:, :, 1, :], op=ALU.subtract),
                       reads=["T1m", "T2m"], writes=[("SEQ", qd)])
                P.dve(lambda e, g4=g4: e.tensor_tensor(out=SEQ[:, g4, 1, :], in0=T1m[:, :, 1, :], in1=T2m[:, :, 0, :], op=ALU.add),
                       reads=["T1m", "T2m"], writes=[("SEQ", qd)])

        def seg_F2(sg):
            t0 = sg * T
            db = sg % 2
            xres_seg = [("xT", sg)]
            cb = carry.unsqueeze(3).to_broadcast([128, 16, 2, 2])
            P.dve(lambda e, cb=cb: e.tensor_tensor(out=P4, in0=A1c, in1=cb, op=ALU.mult), reads=["A1c", "carry"], writes=["P4"])
            P.dve(lambda e: e.tensor_tensor(out=SEQ[:, :, :, 0], in0=SEQ[:, :, :, 0], in1=P4[:, :, 0, :], op=ALU.add),
                  reads=["P4"] + [("SEQ", q) for q in range(4)], writes=[("SEQ", q) for q in range(4)])
            P.dve(lambda e: e.tensor_tensor(out=SEQ[:, :, :, 0], in0=SEQ[:, :, :, 0], in1=P4[:, :, 1, :], op=ALU.add),
                  reads=["P4"] + [("SEQ", q) for q in range(4)], writes=[("SEQ", q) for q in range(4)])
            for qd in range(4):
                sq = SEQ[:, 4 * qd:4 * qd + 4, :, :].rearrange("p a b c -> p (a b c)")
                P.dve(lambda e, sq=sq: e.tensor_tensor_scan(out=sq, data0=ONES, data1=sq, initial=0.0, op0=ALU.mult, op1=ALU.add),
                      reads=[("SEQ", qd), "ONES"], writes=[("SEQ", qd)])
            lb = SEQ[:, :, :, T - 1].unsqueeze(3).to_broadcast([128, 16, 2, 2])
            P.dve(lambda e, lb=lb: e.tensor_tensor(out=P4, in0=ALc, in1=lb, op=ALU.mult), reads=["ALc"] + [("SEQ", q) for q in range(4)], writes=["P4"])
            P.dve(lambda e: e.tensor_tensor(out=carry, in0=P4[:, :, 0, :], in1=P4[:, :, 1, :], op=ALU.add), reads=["P4"], writes=["carry"])
            for qd in range(4):
                g4 = slice(4 * qd, 4 * qd + 4)
                P.act(lambda e, g4=g4: e.copy(out=W16[:, g4, :, :].rearrange("p a b c -> p (a b c)"),
                                              in_=SEQ[:, g4, :, :].rearrange("p a b c -> p (a b c)")),
                      reads=[("SEQ", qd)], writes=[("W16", qd)])

        def seg_B1(sg):
            t0 = sg * T
            db = sg % 2
            xres_seg = [("xT", sg)]
            for qd in range(4):
                g4 = slice(4 * qd, 4 * qd + 4)
                tdr = TDr[:, g4, :].unsqueeze(2).to_broadcast([128, 4, 2, 128])
                tdi = TDi[:, g4, :].unsqueeze(2).to_broadcast([128, 4, 2, 128])
                W4 = W16[:, g4, :, :]
                P.dve(lambda e, W4=W4, tdr=tdr: e.tensor_tensor(out=T1d, in0=W4, in1=tdr, op=ALU.mult), reads=[("W16", qd), "TDr"], writes=["T1d"])
                P.dve(lambda e, W4=W4, tdi=tdi: e.tensor_tensor(out=T2d, in0=W4, in1=tdi, op=ALU.mult), reads=[("W16", qd), "TDi"], writes=["T2d"])
                P.dve(lambda e, g4=g4: e.tensor_tensor(out=Xb[:, g4, 0, :], in0=T1d[:, :, 0, :], in1=T2d[:, :, 1, :], op=ALU.subtract),
                       reads=["T1d", "T2d"], writes=[("Xb", qd)])
                P.dve(lambda e, g4=g4: e.tensor_tensor(out=Xb[:, g4, 1, :], in0=T1d[:, :, 1, :], in1=T2d[:, :, 0, :], op=ALU.add),
                       reads=["T1d", "T2d"], writes=[("Xb", qd)])

        def seg_B2a(sg):
            t0 = sg * T
            db = sg % 2
            xres_seg = [("xT", sg)]
            def fzs(e, t0=t0):
                ins = None
                for ee in range(4):
                    for d_ in range(8):
                        ins = e.matmul(pb[0][:, ee * 128:(ee + 1) * 128], lhsT=w_zs[:, d_, ee * 128:(ee + 1) * 128],
                                       rhs=xT[:, d_, t0:t0 + T], start=(d_ == 0), stop=(d_ == 7))
                return ins
            P.pe(fzs, reads=["w_zs"] + xres_seg, writes=[PS(0)])
            P.act(lambda e: e.activation(out=s2, in_=pb[0][:, :], func=AF.Sigmoid), reads=[PS(0)], writes=["s2"])
            P.dve(lambda e: e.tensor_tensor(out=m2, in0=pb[0][:, :], in1=s2, op=ALU.mult), reads=[PS(0), "s2"], writes=["m2"])
            for gh in range(2):
                ybank = 5 + gh

                def fy(e, gh=gh, ybank=ybank):
                    ins = None
                    pr0 = 64 * gh
                    for pl in range(8):
                        pr = 8 * gh + pl
                        rows = 32 * (pr % 4)
                        c0 = ((pr // 4) % 2) * 128
                        tp = (64 * gh, rows)
                        o32 = pb[ybank][rows:rows + 32, c0:c0 + 128]
                        o16 = pb[ybank][rows:rows + 16, c0:c0 + 128]
                        e.matmul(o32, lhsT=CmS[pr0:pr0 + 64, pl, 0, 16:48], rhs=Xb[pr0:pr0 + 64, 2 * pl + 1, 0, :], start=True, stop=False, tile_position=tp)
                        e.matmul(o32, lhsT=CmS[pr0:pr0 + 64, pl, 1, 16:48], rhs=Xb[pr0:pr0 + 64, 2 * pl + 1, 1, :], start=False, stop=False, tile_position=tp)
                        e.matmul(o16, lhsT=CmS[pr0:pr0 + 64, pl, 0, 0:16], rhs=Xb[pr0:pr0 + 64, 2 * pl, 0, :], start=False, stop=False, tile_position=tp)
                        ins = e.matmul(o16, lhsT=CmS[pr0:pr0 + 64, pl, 1, 0:16], rhs=Xb[pr0:pr0 + 64, 2 * pl, 1, :], start=False, stop=True, tile_position=tp)
                    return ins
                P.pe(fy, reads=["CmS"] + [("Xb", q) for q in range(4)], writes=[PS(ybank)])
                dsh = dsu[db][:, 2 * gh:2 * gh + 2, :].rearrange("p a b -> p (a b)")
                P.dve(lambda e, gh=gh, ybank=ybank, dsh=dsh: e.tensor_tensor(out=tS[:, 256 * gh:256 * gh + 256], in0=pb[ybank][:, 0:256], in1=dsh, op=ALU.add),
                      reads=[PS(ybank), ("dsu", db)], writes=["tS"])
            P.act(lambda e: e.activation(out=sqS, in_=tS, func=AF.Square), reads=["tS"], writes=["sqS"])
            P.pool(lambda e: e.tensor_scalar(out=inS, in0=sqS, scalar1=0.044715, scalar2=1.0, op0=ALU.mult, op1=ALU.add),
                   reads=["sqS"], writes=["inS"])
            P.pool(lambda e: e.tensor_tensor(out=inS, in0=inS, in1=tS, op=ALU.mult), reads=["inS", "tS"], writes=["inS"])
            P.act(lambda e: e.activation(out=sgS, in_=inS, func=AF.Sigmoid, scale=1.5957691216), reads=["inS"], writes=["sgS"])
            yTf = yT.rearrange("p a b -> p (a b)")
            P.pool(lambda e: e.tensor_tensor(out=yTf, in0=tS, in1=sgS, op=ALU.mult), reads=["tS", "sgS"], writes=["yT"])


        def seg_B2b(sg):
            t0 = sg * T
            db = sg % 2
            xres_seg = [("xT", sg)]
            yTf = yT.rearrange("p a b -> p (a b)")
            def fglu(e):
                ins = None
                for ee in range(4):
                    for c_ in range(4):
                        ins = e.matmul(pb[7][:, ee * 128:(ee + 1) * 128], lhsT=w_glu[:, c_, ee * 128:(ee + 1) * 128],
                                       rhs=yT[:, c_, :], start=(c_ == 0), stop=(c_ == 3))
                return ins
            P.pe(fglu, reads=["w_glu", "yT"], writes=[PS(7)])

            for ee in range(4):
                P.act(lambda e, ee=ee: e.activation(out=s1[:, ee, :], in_=pb[7][:, ee * 128:(ee + 1) * 128], func=AF.Sigmoid,
                                                    bias=bglu[:, ee:ee + 1]), reads=[PS(7), "bglu"], writes=["s1"])
            s1f = s1.rearrange("p a b -> p (a b)")
            P.pool(lambda e: e.tensor_tensor(out=m1, in0=yTf, in1=s1f, op=ALU.mult), reads=["yT", "s1"], writes=["m1"])
            yscf = ysc[db].rearrange("p a b -> p (a b)")
            P.pool(lambda e, yscf=yscf: e.tensor_tensor(out=yscf, in0=m1, in1=m2, op=ALU.mult), reads=["m1", "m2"], writes=[("ysc", db)])
            P.dma("sp", ys_d[:, :, t0:t0 + T], ysc[db], ("ysc", db), reads=[("ysc", db)], writes=[("ys_d", sg)])


        if NSEG:
            seg_F1(0)
            seg_F2(0)
        for sg in range(NSEG):
            if sg + 1 < NSEG:
                seg_F1(sg + 1)
            if sg >= 1:
                seg_B2b(sg - 1)
            seg_B1(sg)
            if sg + 1 < NSEG:
                seg_F2(sg + 1)
            seg_B2a(sg)
        if NSEG:
            seg_B2b(NSEG - 1)
        S_NAMES = (S_ALL + SEG_NAMES + ["Bq16", "yT", "s1", "s2", "m1", "m2"]
                   + [("ysc", i) for i in range(2)])

        CA = Carver()
        qT = CA.get("qT", [128, L], BF16)
        kT = CA.get("kT", [128, L], BF16)
        vaug = CA.get("vaug", [128, 32, 2, 128], BF16)
        acc = [CA.get("acc%d" % i, [128, L], F32) for i in range(2)]
        wq2 = [CA.get("wq%d" % i, [128, 8, 128], BF16) for i in range(2)]
        wk2 = [CA.get("wk%d" % i, [128, 8, 128], BF16) for i in range(2)]
        wv2 = [CA.get("wv%d" % i, [128, 8, 128], BF16) for i in range(2)]
        mk2 = [CA.get("mk%d" % i, [128, 2, 256], BF16) for i in range(2)]
        wza = CA.get("wza", [128, 8, 128], BF16)
        Pe = [CA.get("Pe%d" % i, [128, 256], BF16) for i in range(2)]
        Pm = [CA.get("Pm%d" % i, [128, 256], BF16) for i in range(6)]
        sgz = CA.get("sgz", [128, 512], F32)
        szt = CA.get("szt", [128, 512], F32)
        rLt = CA.get("rLt", [128, 512], F32)
        o1t = CA.get("o1t", [128, 512], F32)
        A_NAMES = (["qT", "kT", "vaug", "vones", "acc0", "acc1", "wza", "sgz", "szt", "rLt", "o1t"]
                   + [("Pe", i) for i in range(2)] + [("Pm", i) for i in range(6)]
                   + [(n, i) for n in ("wq", "wk", "wv", "mk") for i in range(2)])
        P.alias(A_NAMES, S_NAMES)
        P.alias(["attnT"], XS_NAMES)
        P.pool(lambda e: e.memset(vaug[:, :, :, 64:128], 1.0), writes=["vones"])
        rot = {"p": 0, "s": 0, "o": 0, "e": 0, "m": 0}
        for hp in range(4 if 'A' in STAGES else 0):
            for g in range(3):
                r = DIL[g]
                nb = 32 // r
                ws = (3 * hp + g) % 2
                wq, wk, wv, mk = wq2[ws], wk2[ws], wv2[ws], mk2[ws]
                wqn, wkn, wvn, mkn = ("wq", ws), ("wk", ws), ("wv", ws), ("mk", ws)
                load_w(wq, w_in[:, g * 512 + hp * 128: g * 512 + hp * 128 + 128], wqn, wqn)
                load_w(wk, w_in[:, 1536 + g * 512 + hp * 128: 1536 + g * 512 + hp * 128 + 128], wkn, wkn)
                load_w(wv, w_in[:, 3072 + g * 512 + hp * 128: 3072 + g * 512 + hp * 128 + 128], wvn, wvn)
                P.dma("pool", mk, mask_d[g * 8 + 2 * hp: g * 8 + 2 * hp + 2].rearrange("h p q -> p h q"), mkn, writes=[mkn])
                for (wt, wres, dst, dres) in ((wq, wqn, qT, "qT"), (wk, wkn, kT, "kT")):
                    for c in range(8):
                        bank = rot["p"] % 2
                        rot["p"] += 1

                        def fqk(e, wt=wt, c=c, bank=bank):
                            ins = None
                            for d_ in range(8):
                                ins = e.matmul(pb[bank][:, :], lhsT=wt[:, d_, :], rhs=xT[:, d_, c * 512:(c + 1) * 512],
                                               start=(d_ == 0), stop=(d_ == 7))
                            return ins
                        P.pe(fqk, reads=[wres] + XT_ALL[4 * c:4 * c + 4], writes=[PS(bank)])
                        P.act(lambda e, dst=dst, c=c, bank=bank: e.copy(out=dst[:, c * 512:(c + 1) * 512], in_=pb[bank][:, :]),
                              reads=[PS(bank)], writes=[dres])
                for vq in range(8):
                    bank = rot["p"] % 2
                    rot["p"] += 1

                    def fv(e, vq=vq, bank=bank, r=r, nb=nb, wv=wv):
                        ins = None
                        for j in range(4):
                            tile = 4 * vq + j
                            ph, jb = divmod(tile, nb)
                            base = 128 * jb * r + ph
                            for d_ in range(8):
                                ins = e.matmul(pb[bank][:, j * 128:(j + 1) * 128], lhsT=xT[:, d_, base:base + 127 * r + 1:r],
                                               rhs=wv[:, d_, :], start=(d_ == 0), stop=(d_ == 7))
                        return ins
                    P.pe(fv, reads=[wvn] + XT_ALL, writes=[PS(bank)])
                    for hh in range(2):
                        srcv = pb[bank][:, :].rearrange("p (a h b) -> p a h b", a=4, h=2)[:, :, hh, :]
                        if vq % 2 == 0:
                            P.act(lambda e, vq=vq, srcv=srcv, hh=hh: e.copy(out=vaug[:, 4 * vq:4 * vq + 4, hh, 0:64], in_=srcv),
                                  reads=[PS(bank)], writes=["vaug"])
                        else:
                            P.dve(lambda e, vq=vq, srcv=srcv, hh=hh: e.tensor_copy(out=vaug[:, 4 * vq:4 * vq + 4, hh, 0:64], in_=srcv),
                                  reads=[PS(bank)], writes=["vaug"])
                for hh in range(2):
                    hr = 64 * hh
                    accn = "acc%d" % hh
                    units = [(ph, jb) for ph in range(r) for jb in range(nb)]
                    LA = 3
                    pm_of = {}
                    st8 = {"obank": None, "qf": None}

                    def emit_s(ui, hh=hh, hr=hr):
                        ph, jb = units[ui]
                        base = 128 * jb * r + ph
                        nq = 256 if jb < nb - 1 else 128
                        sbank = (2, 3, 6, 7)[rot["s"] % 4]
                        rot["s"] += 1
                        pe_i = rot["e"] % 2
                        rot["e"] += 1
                        pm_i = rot["m"] % 6
                        rot["m"] += 1
                        pm_of[ui] = pm_i

                        def fs(e, base=base, nq=nq, sbank=sbank, r=r, hr=hr):
                            return e.matmul(pb[sbank][:, 0:nq], lhsT=kT[hr:hr + 64, base:base + 127 * r + 1:r],
                                            rhs=qT[hr:hr + 64, base:base + (nq - 1) * r + 1:r], start=True, stop=True)
                        P.pe(fs, reads=["qT", "kT"], writes=[PS(sbank)])
                        P.act(lambda e, pe_i=pe_i, nq=nq, sbank=sbank: e.activation(
                            out=Pe[pe_i][:, 0:nq], in_=pb[sbank][:, 0:nq], func=AF.Exp, scale=0.125),
                            reads=[PS(sbank)], writes=[("Pe", pe_i)])
                        P.dve(lambda e, pe_i=pe_i, pm_i=pm_i, nq=nq, mk=mk: e.tensor_tensor(
                            out=Pm[pm_i][:, 0:nq], in0=Pe[pe_i][:, 0:nq], in1=mk[:, hh, 0:nq], op=ALU.mult),
                            reads=[("Pe", pe_i), mkn], writes=[("Pm", pm_i)])

                    def emit_pv(ui, hh=hh, accn=accn):
                        ph, jb = units[ui]
                        oq = ui % 4
                        if oq == 0:
                            st8["obank"] = 4 + rot["o"] % 2
                            rot["o"] += 1
                            st8["qf"] = (ph, jb)
                        obank = st8["obank"]
                        tile = ph * nb + jb
                        pm_i = pm_of[ui]
                        prev_pm = pm_of.get(ui - 1)

                        def fpv(e, obank=obank, oq=oq, tile=tile, pm_i=pm_i, jb=jb, prev_pm=prev_pm):
                            ins = e.matmul(pb[obank][:, oq * 128:(oq + 1) * 128], lhsT=vaug[:, tile, hh, :],
                                           rhs=Pm[pm_i][:, 0:128], start=True, stop=(jb == 0))
                            if jb > 0:
                                ins = e.matmul(pb[obank][:, oq * 128:(oq + 1) * 128], lhsT=vaug[:, tile - 1, hh, :],
                                               rhs=Pm[prev_pm][:, 128:256], start=False, stop=True)
                            return ins
                        rd = ["vaug", "vones", ("Pm", pm_i)] + ([("Pm", prev_pm)] if jb > 0 else [])
                        P.pe(fpv, reads=rd, writes=[PS(obank)])
                        if oq == 3:
                            ph0, jb0 = st8["qf"]
                            a = acc[hh]
                            if r == 1:
                                av = a[:, 128 * jb0:128 * jb0 + 512]
                                sv = pb[obank][:, :]
                            elif r == 4:
                                av = a[:, :].rearrange("p (j i f) -> p f j i", i=128, f=4)[:, ph0, jb0:jb0 + 4, :]
                                sv = pb[obank][:, :].rearrange("p (j i) -> p j i", j=4)
                            else:
                                av = a[:, :].rearrange("p (j i f) -> p f j i", j=2, f=16)[:, ph0:ph0 + 2, :, :]
                                sv = pb[obank][:, :].rearrange("p (f j i) -> p f j i", f=2, j=2)
                            if g == 0:
                                P.dve(lambda e, av=av, sv=sv: e.tensor_copy(out=av, in_=sv), reads=[PS(obank)], writes=[accn])
                            else:
                                P.dve(lambda e, av=av, sv=sv: e.tensor_tensor(out=av, in0=sv, in1=av, op=ALU.add),
                                      reads=[PS(obank), accn], writes=[accn])

                    for i in range(len(units) + LA):
                        if i < len(units):
                            emit_s(i)
                        if i >= LA:
                            emit_pv(i - LA)
            load_w(wza, w_in[:, 4608 + hp * 128:4608 + hp * 128 + 128], "wza", "wza")
            for c in range(8):
                bank = rot["p"] % 2
                rot["p"] += 1
                cs = slice(c * 512, (c + 1) * 512)

                def fza(e, c=c, bank=bank):
                    ins = None
                    for d_ in range(8):
                        ins = e.matmul(pb[bank][:, :], lhsT=wza[:, d_, :], rhs=xT[:, d_, c * 512:(c + 1) * 512],
                                       start=(d_ == 0), stop=(d_ == 7))
                    return ins
                P.pe(fza, reads=["wza"] + XT_ALL[4 * c:4 * c + 4], writes=[PS(bank)])
                P.act(lambda e, bank=bank: e.activation(out=sgz, in_=pb[bank][:, :], func=AF.Sigmoid), reads=[PS(bank)], writes=["sgz"])
                P.dve(lambda e, bank=bank: e.tensor_tensor(out=szt, in0=pb[bank][:, :], in1=sgz, op=ALU.mult),
                      reads=[PS(bank), "sgz"], writes=["szt"])
                P.dve(lambda e, cs=cs: e.tensor_copy(out=rLt[0:64, :], in_=acc[0][64:128, cs]), reads=["acc0"], writes=["rLt"])
                P.dve(lambda e, cs=cs: e.tensor_copy(out=rLt[64:128, :], in_=acc[1][64:128, cs]), reads=["acc1"], writes=["rLt"])
                P.dve(lambda e: e.reciprocal(out=rLt, in_=rLt), reads=["rLt"], writes=["rLt"])
                P.dve(lambda e, cs=cs: e.tensor_copy(out=o1t[0:64, :], in_=acc[0][0:64, cs]), reads=["acc0"], writes=["o1t"])
                P.dve(lambda e, cs=cs: e.tensor_copy(out=o1t[64:128, :], in_=acc[1][0:64, cs]), reads=["acc1"], writes=["o1t"])
                P.pool(lambda e: e.tensor_tensor(out=o1t, in0=o1t, in1=rLt, op=ALU.mult), reads=["o1t", "rLt"], writes=["o1t"])
                P.pool(lambda e, hp=hp, cs=cs: e.tensor_tensor(out=attnT[:, hp, cs], in0=o1t, in1=szt, op=ALU.mult),
                       reads=["o1t", "szt"], writes=["attnT"])
        if dbg:
            P.dma("sp", attn_dbg, attnT[:, :, :], "attn_dbg", reads=["attnT"], writes=["attn_dbg"])

        CM = Carver()
        w_au = CM.get("w_au", [128, 4, D], BF16)
        w_su = CM.get("w_su", [128, 4, D], BF16)
        w_o = CM.get("w_o", [128, 8, D], BF16)
        wga = CM.get("wga", [128, 8, D], BF16)
        wgs = CM.get("wgs", [128, 8, D], BF16)
        merged = CM.get("merged", [128, 8, 512], BF16)
        yscM = CM.get("yscM", [128, 4, 512], BF16)
        sgb = [CM.get("sgb%d" % i, [128, 512], BF16) for i in range(3)]
        mma = [CM.get("mma%d" % i, [128, 512], BF16) for i in range(2)]
        mmb = [CM.get("mmb%d" % i, [128, 512], BF16) for i in range(2)]
        xres = [CM.get("xres%d" % i, [128, D], F32) for i in range(2)]
        hbuf = [CM.get("hbuf%d" % i, [128, D], F32) for i in range(2)]
        lng = CM.get("lng", [128, D], F32)
        lnb = CM.get("lnb", [128, D], F32)
        M_NAMES = (["w_au", "w_su", "w_o", "wga", "wgs", "merged", "yscM", "lng", "lnb"]
                   + [(("hbuf", i), eh) for i in range(2) for eh in range(2)]
                   + [("stats", i, eh) for i in range(2) for eh in range(2)] + [(n, i) for n in ("mv", "rstd", "nb") for i in range(2)]
                   + [("sgb", i) for i in range(3)]
                   + [(n, i) for n in ("mma", "mmb", "xres") for i in range(2)])
        P.alias(M_NAMES, A_NAMES + S_NAMES)
        P.dma("sp", lng, lng_d, "lng", writes=["lng"])
        P.dma("sp", lnb, lnb_d, "lnb", writes=["lnb"])
        load_w(w_au, w_au_d, "w_au", "w_au")
        load_w(w_su, w_su_d, "w_su", "w_su")
        load_w(wga, w_in[:, 6144:7168], "wga", "wga")
        load_w(wgs, w_in[:, 7168:8192], "wgs", "wgs")
        load_w(w_o, w_o_d, "w_o", "w_o")
        stats2 = [CM.get("stats%d" % i, [128, 12], F32) for i in range(2)]
        mv2 = [CM.get("mv%d" % i, [128, 2], F32) for i in range(2)]
        rstd2 = [CM.get("rstd%d" % i, [128, 1], F32) for i in range(2)]
        ln_queue = []

        def ln_front(c, t4):
            tt_ = 4 * c + t4
            xi = tt_ % 2
            hb, hbn = hbuf[xi], ("hbuf", xi)
            st_, mv_, rs_ = stats2[xi], mv2[xi], rstd2[xi]
            if tt_ == 0:
                P.dma("sp", xres[0], x[0:128, :], ("xres", 0), writes=[("xres", 0)])
            if tt_ + 1 < 32:
                nx = (tt_ + 1) % 2
                P.dma("sp", xres[nx], x[(tt_ + 1) * 128:(tt_ + 2) * 128, :], ("xres", nx), writes=[("xres", nx)])
            banks = []
            for eh in range(2):
                bank = 6 + rm["b"] % 2
                rm["b"] += 1
                banks.append(bank)

                def fo(e, t4=t4, eh=eh, bank=bank):
                    ins = None
                    for dm in range(8):
                        ins = e.matmul(pb[bank][:, :], lhsT=merged[:, dm, t4 * 128:(t4 + 1) * 128], rhs=w_o[:, dm, eh * 512:(eh + 1) * 512],
                                       start=(dm == 0), stop=(dm == 7))
                    return ins
                P.pe(fo, reads=["merged", "w_o"], writes=[PS(bank)])
            for eh in range(2):
                bank = banks[eh]
                P.dve(lambda e, xi=xi, eh=eh, bank=bank, hb=hb: e.scalar_tensor_tensor(
                    out=hb[:, eh * 512:(eh + 1) * 512], in0=xres[xi][:, eh * 512:(eh + 1) * 512], scalar=ALPHA,
                    in1=pb[bank][:, :], op0=ALU.mult, op1=ALU.add), reads=[("xres", xi), PS(bank)], writes=[(hbn, eh)])
            for eh in range(2):
                P.dve(lambda e, eh=eh, hb=hb, st_=st_: e.bn_stats(out=st_[:, eh * 6:(eh + 1) * 6], in_=hb[:, eh * 512:(eh + 1) * 512]),
                      reads=[(hbn, eh)], writes=[("stats", xi, eh)])
            P.dve(lambda e, st_=st_, mv_=mv_: e.bn_aggr(out=mv_, in_=st_), reads=[("stats", xi, 0), ("stats", xi, 1)], writes=[("mv", xi)])
            P.dve(lambda e, mv_=mv_, rs_=rs_: e.tensor_scalar(out=rs_, in0=mv_[:, 1:2], scalar1=EPS, scalar2=None, op0=ALU.add),
                  reads=[("mv", xi)], writes=[("rstd", xi)])
            P.act(lambda e, rs_=rs_: e.activation(out=rs_, in_=rs_, func=AF.Sqrt), reads=[("rstd", xi)], writes=[("rstd", xi)])

        nb2 = [CM.get("nb%d" % i, [128, 1], F32) for i in range(2)]

        def ln_back(c, t4):
            tt_ = 4 * c + t4
            xi = tt_ % 2
            hb, hbn = hbuf[xi], ("hbuf", xi)
            mv_, rs_, nb_ = mv2[xi], rstd2[xi], nb2[xi]
            P.dve(lambda e, rs_=rs_: e.reciprocal(out=rs_, in_=rs_), reads=[("rstd", xi)], writes=[("rstd", xi)])
            P.dve(lambda e, mv_=mv_, rs_=rs_, nb_=nb_: e.tensor_scalar(out=nb_, in0=mv_[:, 0:1], scalar1=rs_[:, 0:1], scalar2=-1.0,
                                                                    op0=ALU.mult, op1=ALU.mult), reads=[("mv", xi), ("rstd", xi)], writes=[("nb", xi)])
            for eh in range(2):
                hs = hb[:, eh * 512:(eh + 1) * 512]
                hr = [(hbn, eh)]
                P.act(lambda e, hs=hs, rs_=rs_, nb_=nb_: e.activation(out=hs, in_=hs, func=AF.Identity, scale=rs_[:, 0:1], bias=nb_[:, 0:1]),
                      reads=hr + [("rstd", xi), ("nb", xi)], writes=hr)
                eng = P.pool if eh == 0 else P.dve
                eng(lambda e, hs=hs, eh=eh: e.tensor_tensor(out=hs, in0=hs, in1=lng[:, eh * 512:(eh + 1) * 512], op=ALU.mult), reads=hr + ["lng"], writes=hr)
                eng(lambda e, hs=hs, eh=eh: e.tensor_tensor(out=hs, in0=hs, in1=lnb[:, eh * 512:(eh + 1) * 512], op=ALU.add), reads=hr + ["lnb"], writes=hr)
            P.dma("sp", out_d[tt_ * 128:(tt_ + 1) * 128, :], hb, hbn, reads=[(hbn, 0), (hbn, 1)], writes=["out_d"])

        rm = {"a": 0, "b": 0, "g": 0, "x": 0, "m": 0}
        for c in range(8 if 'M' in STAGES else 0):
            cs = slice(c * 512, (c + 1) * 512)
            P.dma("sp", yscM, ys_d[:, :, cs], "yscM", reads=[("ys_d", 4 * c + i) for i in range(4)], writes=["yscM"])
            for dm in range(8):
                mi = rm["m"] % 2
                rm["m"] += 1
                for br in range(2):
                    slot = rm["a"] % 3
                    rm["a"] += 1
                    b0, b1 = 2 * slot, 2 * slot + 1
                    gi = rm["g"] % 3
                    rm["g"] += 1
                    wup, wupn, act_t, actn, wgt, wgn = ((w_au, "w_au", attnT, "attnT", wga, "wga") if br == 0
                                                        else (w_su, "w_su", None, "yscM", wgs, "wgs"))

                    def fbr(e, dm=dm, c=c, br=br, b0=b0, b1=b1, wup=wup, wgt=wgt):
                        ins = None
                        for cp in range(4):
                            rhs = attnT[:, cp, c * 512:(c + 1) * 512] if br == 0 else yscM[:, cp, :]
                            e.matmul(pb[b0][:, :], lhsT=wup[:, cp, dm * 128:(dm + 1) * 128], rhs=rhs, start=(cp == 0), stop=(cp == 3))
                        for d_ in range(8):
                            ins = e.matmul(pb[b1][:, :], lhsT=wgt[:, d_, dm * 128:(dm + 1) * 128], rhs=xT[:, d_, c * 512:(c + 1) * 512],
                                           start=(d_ == 0), stop=(d_ == 7))
                        return ins
                    P.pe(fbr, reads=[wupn, actn, wgn] + XT_ALL[4 * c:4 * c + 4], writes=[PS(b0), PS(b1)])
                    P.act(lambda e, b1=b1, gi=gi: e.activation(out=sgb[gi], in_=pb[b1][:, :], func=AF.Sigmoid),
                          reads=[PS(b1)], writes=[("sgb", gi)])
                    dstm = mma[mi] if br == 0 else mmb[mi]
                    dstn = ("mma", mi) if br == 0 else ("mmb", mi)
                    P.dve(lambda e, b0=b0, gi=gi, dstm=dstm: e.tensor_tensor(out=dstm, in0=pb[b0][:, :], in1=sgb[gi], op=ALU.mult),
                          reads=[PS(b0), ("sgb", gi)], writes=[dstn])
                P.pool(lambda e, dm=dm, mi=mi: e.tensor_tensor(out=merged[:, dm, :], in0=mma[mi], in1=mmb[mi], op=ALU.add),
                       reads=[("mma", mi), ("mmb", mi)], writes=["merged"])
            for t4 in range(4):
                ln_queue.append((c, t4))
                if len(ln_queue) >= 2:
                    ln_front(*ln_queue[-1])
                    ln_back(*ln_queue[-2])
                else:
                    ln_front(*ln_queue[-1])
        if ln_queue:
            ln_back(*ln_queue[-1])
        fin = [(("hbuf", i), eh) for i in range(2) for eh in range(2)] + ["out_d"] + (["attn_dbg"] if dbg else [])
        P.add("sp", None, reads=fin, writes=fin)
        P.emit()
    return nc


_CACHE = {}


def _consts():
    ident = np.eye(128, dtype=np.float32)
    rr = np.arange(128)
    rowmask = (rr[:, None] // 16 == np.arange(8)[None, :]).astype(np.float32)
    kk = np.arange(128)[:, None]
    qq = np.arange(256)[None, :]
    dist = qq - kk
    valid = (dist >= 0) & (dist <= 128)
    slopes = 2.0 ** (-8.0 * np.arange(1, 9, dtype=np.float64) / 8)
    mask = np.zeros((3, 8, 128, 256), np.float32)
    for g, r in enumerate(DIL):
        for h in range(8):
            mask[g, h] = np.where(valid, np.exp(-slopes[h] * r * np.maximum(dist, 0)), 0.0)
    return ident, rowmask, mask.reshape(24, 128, 256)


def _layout_params(inp):
    lam_re, lam_im, log_dt = inp["lam_re"][0], inp["lam_im"][0], inp["log_dt"][0]
    b_re, b_im, c_re, c_im = inp["b_re"][0], inp["b_im"][0], inp["c_re"][0], inp["c_im"][0]
    r = np.arange(128)
    gl, cc = r // 16, r % 16
    m = {}
    gB = 8 * np.arange(4)[None, :] + gl[:, None]
    m["pB_lr"] = lam_re[gB].reshape(128, 256)
    m["pB_li"] = lam_im[gB].reshape(128, 256)
    m["pB_ldt"] = np.repeat(log_dt[gB][:, :, None], 64, axis=2).reshape(128, 256)
    m["pB_br"] = b_re[gB, :, cc[:, None]].reshape(128, 256)
    m["pB_bi"] = b_im[gB, :, cc[:, None]].reshape(128, 256)
    cre = c_re.reshape(2, 16, 16, 64).transpose(0, 3, 1, 2)
    cim = c_im.reshape(2, 16, 16, 64).transpose(0, 3, 1, 2)
    pc = np.stack([cre, cim], axis=3).reshape(128, 16, 2, 16)
    m["pC_c"] = pc.reshape(128, 512)
    gh, pp = r // 64, r % 64
    gS = 16 * gh[:, None] + np.arange(16)[None, :]
    m["pS_lr"] = lam_re[gS, pp[:, None]]
    m["pS_li"] = lam_im[gS, pp[:, None]]
    m["pS_ldt"] = log_dt[gS]
    m["dsk"] = inp["d_skip"][0].reshape(4, 128).T
    m["bglu"] = inp["b_glu"][0].reshape(4, 128).T
    m["lng"] = np.repeat(inp["ln_g"][0][None, :], 128, axis=0)
    m["lnb"] = np.repeat(inp["ln_b"][0][None, :], 128, axis=0)
    return {k: np.ascontiguousarray(v, dtype=np.float32) for k, v in m.items()}


def kernel(dbg=False, **inp):
    inp = {k: np.asarray(v) for k, v in inp.items()}
    key = bool(dbg)
    if key not in _CACHE:
        _CACHE[key] = build(dbg)
    nc = _CACHE[key]
    ident, rowmask, mask = _consts()
    shared = _layout_params(inp)
    shared.update({
        "w_in": np.ascontiguousarray(inp["w_in"][0]), "w_glu": np.ascontiguousarray(inp["w_glu"][0]),
        "w_attn_up": np.ascontiguousarray(inp["w_attn_up"][0]), "w_ssm_up": np.ascontiguousarray(inp["w_ssm_up"][0]),
        "w_o": np.ascontiguousarray(inp["w_o"][0]), "c_ident": ident, "c_rowmask": rowmask, "c_mask": mask,
    })
    nb = inp["x"].shape[0]
    in_maps = []
    for b in range(nb):
        d = dict(shared)
        d["x"] = np.ascontiguousarray(inp["x"][b])
        in_maps.append(d)
    res = run_bass_kernel_spmd(nc, in_maps, core_ids=list(range(nb)))
    out = np.stack([np.asarray(res.results[b]["out"]) for b in range(nb)], axis=0).astype(np.float32)
    if dbg:
        return out, res
    return out
```

```python
import math
import os
STAGES = os.environ.get('KSTAGES', 'XSAM')
CUT = int(os.environ.get('KCUT', '99'))
STRICT = os.environ.get('KSTRICT', '1') == '1'
from contextlib import ExitStack

import numpy as np
import concourse.bass as bass
import concourse.mybir as mybir
from concourse.bass_utils import run_bass_kernel_spmd

F32 = mybir.dt.float32
BF16 = mybir.dt.bfloat16
AF = mybir.ActivationFunctionType
ALU = mybir.AluOpType

L = 4096
D = 1024
DIL = (1, 4, 16)
ALPHA = 2.0 ** 0.25
EPS = 1e-5


class _Op:
    __slots__ = ("eng", "fn", "reads", "writes", "dma_key", "deps", "inc", "cnt", "dma_cnt")

    def __init__(self, eng, fn, reads, writes, dma_key):
        self.eng = eng
        self.fn = fn
        self.reads = tuple(reads)
        self.writes = tuple(writes)
        self.dma_key = dma_key
        self.deps = ()
        self.inc = False
        self.cnt = 0
        self.dma_cnt = 0


class Prog:
    ENGS = ("pe", "act", "dve", "pool", "sp")

    def __init__(self, nc):
        self.nc = nc
        self.ops = []

    def add(self, eng, fn, reads=(), writes=(), dma_key=None):
        self.ops.append(_Op(eng, fn, reads, writes, dma_key))

    def pe(self, fn, reads=(), writes=()):
        self.add("pe", fn, reads, writes)

    def act(self, fn, reads=(), writes=()):
        self.add("act", fn, reads, writes)

    def dve(self, fn, reads=(), writes=()):
        self.add("dve", fn, reads, writes)

    def pool(self, fn, reads=(), writes=()):
        self.add("pool", fn, reads, writes)

    def dma(self, q, out, in_, key, reads=(), writes=(), **kw):
        self.add(q, lambda e: e.dma_start(out=out, in_=in_, **kw), reads, writes, dma_key=key)

    def alias(self, new, old):
        self.ops.append(("alias", tuple(new), tuple(old)))

    def analyze(self):
        last_w = {}
        readers = {}
        real = []
        for item in self.ops:
            if isinstance(item, tuple):
                _, new, old = item
                users = set()
                for o in old:
                    if last_w.get(o) is not None:
                        users.add(last_w[o])
                    users.update(readers.get(o, ()))
                for r in new:
                    last_w[r] = None
                    readers[r] = list(users)
                continue
            op = item
            i = len(real)
            real.append(op)
            deps = set()
            for r in op.reads:
                w = last_w.get(r)
                if w is not None:
                    deps.add(w)
                if isinstance(r, tuple) and r[0] == "ps":
                    for j in readers.get(r, ()):
                        if real[j].eng != op.eng:
                            deps.add(j)
            for r in op.writes:
                w = last_w.get(r)
                if w is not None:
                    deps.add(w)
                deps.update(readers.get(r, ()))
            keep = []
            for j in deps:
                if j == i:
                    continue
                oj = real[j]
                if oj.dma_key is None and op.dma_key is None and oj.eng == op.eng:
                    if op.eng == "pe":
                        continue
                    if not STRICT and not any(r in oj.writes for r in op.reads):
                        continue
                keep.append(j)
            op.deps = tuple(keep)
            for r in op.reads:
                readers.setdefault(r, []).append(i)
            for r in op.writes:
                last_w[r] = i
                readers[r] = []
        self.real = real
        for op in real:
            for j in op.deps:
                if real[j].dma_key is None:
                    real[j].inc = True
        cnt = {e: 0 for e in self.ENGS}
        dcnt = {}
        for op in real:
            if op.dma_key is not None:
                dcnt[op.dma_key] = dcnt.get(op.dma_key, 0) + 1
                op.dma_cnt = dcnt[op.dma_key]
            elif op.inc:
                cnt[op.eng] += 1
                op.cnt = cnt[op.eng]
        self.dma_keys = list(dcnt.keys())

    def emit(self):
        nc = self.nc
        self.analyze()
        ops = self.real
        with ExitStack() as st:
            esem = {e: st.enter_context(nc.semaphore("s_" + e)) for e in self.ENGS}
            dsem = {}
            for k in self.dma_keys:
                dsem[k] = st.enter_context(nc.semaphore("d_%d" % len(dsem)))
            block = st.enter_context(nc.Block())

            def run(eng_name, eng):
                waited = {}
                for op in ops:
                    if op.eng != eng_name:
                        continue
                    need = {}
                    for j in op.deps:
                        oj = ops[j]
                        if oj.dma_key is not None:
                            sk, val = ("d", oj.dma_key), 16 * oj.dma_cnt
                        else:
                            sk, val = ("e", oj.eng), oj.cnt
                        if val > need.get(sk, 0):
                            need[sk] = val
                    for sk, val in need.items():
                        if waited.get(sk, 0) >= val:
                            continue
                        waited[sk] = val
                        sem = dsem[sk[1]] if sk[0] == "d" else esem[sk[1]]
                        eng.wait_ge(sem, val)
                    ins = op.fn(eng) if op.fn is not None else None
                    if op.dma_key is not None:
                        ins.then_inc(dsem[op.dma_key], 16)
                    elif op.inc:
                        ins.then_inc(esem[eng_name], 1)

            @block.tensor
            def _(e):
                run("pe", e)

            @block.scalar
            def _(e):
                run("act", e)

            @block.vector
            def _(e):
                run("dve", e)

            @block.gpsimd
            def _(e):
                run("pool", e)

            @block.sync
            def _(e):
                run("sp", e)


def build(dbg=False):
    nc = bass.Bass("TRN2", target_bir_lowering=False)

    def din(name, shape, dt=F32):
        return nc.dram_tensor(name, list(shape), dt, kind="ExternalInput").ap()

    x = din("x", [L, D])
    w_in = din("w_in", [D, 8192])
    w_glu_d = din("w_glu", [512, 512])
    w_au_d = din("w_attn_up", [512, D])
    w_su_d = din("w_ssm_up", [512, D])
    w_o_d = din("w_o", [D, D])
    pB_lr = din("pB_lr", [128, 256])
    pB_li = din("pB_li", [128, 256])
    pB_ldt = din("pB_ldt", [128, 256])
    pB_br = din("pB_br", [128, 256])
    pB_bi = din("pB_bi", [128, 256])
    pC_c = din("pC_c", [128, 512])
    pS_lr = din("pS_lr", [128, 16])
    pS_li = din("pS_li", [128, 16])
    pS_ldt = din("pS_ldt", [128, 16])
    dsk_d = din("dsk", [128, 4])
    bglu_d = din("bglu", [128, 4])
    lng_d = din("lng", [128, D])
    lnb_d = din("lnb", [128, D])
    ident_d = din("c_ident", [128, 128])
    rowmask_d = din("c_rowmask", [128, 8])
    mask_d = din("c_mask", [24, 128, 256])
    out_d = nc.dram_tensor("out", [L, D], F32, kind="ExternalOutput").ap()
    ys_d = nc.dram_tensor("ys_scr", [128, 4, L], BF16, kind="ExternalOutput" if dbg else "Internal").ap()
    if dbg:
        attn_dbg = nc.dram_tensor("attn_dbg", [128, 4, L], BF16, kind="ExternalOutput").ap()

    P = Prog(nc)
    with ExitStack() as st:
        def sb(name, shape, dt):
            return st.enter_context(nc.sbuf_tensor(name, list(shape), dt))

        pbig = st.enter_context(nc.psum_tensor("pbig", [128, 4096], F32))[:, :]
        pb = [pbig[:, 512 * i:512 * (i + 1)] for i in range(8)]
        PS = lambda i: ("ps", i)

        xT = sb("xT", [128, 8, L], BF16)
        attnT = sb("attnT", [128, 4, L], BF16)
        ident = sb("ident", [128, 128], F32)
        rowmask = sb("rowmask", [128, 8], F32)
        dsk = sb("dskt", [128, 4], F32)
        bglu = sb("bglut", [128, 4], F32)
        ARENA_B = 108 * 1024
        arena = sb("arena", [128, ARENA_B // 2], BF16)

        class Carver:
            def __init__(self):
                self.off = 0
                self.names = []

            def get(self, name, shape, dt):
                n = int(np.prod(shape[1:]))
                nbytes = n * (4 if dt == F32 else 2)
                nbytes = (nbytes + 63) // 64 * 64
                a = arena[:, self.off // 2:(self.off + nbytes) // 2]
                self.off += nbytes
                assert self.off <= ARENA_B, (name, self.off)
                if dt == F32:
                    a = a.bitcast(F32)
                a = a[:, 0:n]
                if len(shape) == 3:
                    a = a.rearrange("p (a b) -> p a b", a=shape[1])
                elif len(shape) == 4:
                    a = a.rearrange("p (a b c) -> p a b c", a=shape[1], b=shape[2])
                self.names.append(name)
                return a

        P.dma("sp", ident[:], ident_d, "ident", writes=["ident"])
        P.dma("sp", rowmask[:], rowmask_d, "rowmask", writes=["rowmask"])
        P.dma("sp", dsk[:], dsk_d, "dsk", writes=["dsk"])
        P.dma("sp", bglu[:], bglu_d, "bglu", writes=["bglu"])

        NXS = 4
        xs = [attnT[:, i, 0:2 * D].bitcast(F32) for i in range(NXS)]
        for tt in range(32):
            sl = tt % NXS
            P.dma("sp", xs[sl], x[tt * 128:(tt + 1) * 128, :], ("xs", sl), writes=[("xs", sl)])
            for h in range(2):
                bank = (2 * tt + h) % 4

                def ftr(e, sl=sl, h=h, bank=bank):
                    ins = None
                    for j in range(4):
                        dt_ = 4 * h + j
                        ins = e.transpose(out=pb[bank][:, j * 128:(j + 1) * 128],
                                          in_=xs[sl][:, dt_ * 128:(dt_ + 1) * 128], identity=ident[:])
                    return ins

                P.pe(ftr, reads=[("xs", sl), "ident"], writes=[PS(bank)])
                P.act(lambda e, h=h, bank=bank, tt=tt: e.copy(
                    out=xT[:, 4 * h:4 * h + 4, tt * 128:(tt + 1) * 128],
                    in_=pb[bank][:, :].rearrange("p (a b) -> p a b", a=4)),
                    reads=[PS(bank)], writes=[("xT", tt)])
        XT_ALL = [("xT", tt) for tt in range(32)]

        def load_w(dst, src_ap, key, res):
            P.dma("pool", dst, src_ap.rearrange("(k p) c -> p k c", p=128), key, writes=[res])

        CS = Carver()
        w_u = CS.get("w_u", [128, 8, 512], BF16)
        w_zs = CS.get("w_zs", [128, 8, 512], BF16)
        w_glu = CS.get("w_glu", [128, 4, 512], BF16)
        BbT = CS.get("BbT", [128, 4, 128], BF16)
        BbTpad = CS.get("BbTpad", [128, 32, 128], BF16)
        CmS = CS.get("CmS", [128, 8, 2, 48], BF16)
        TMr = CS.get("TMr", [128, 16, 128], BF16)
        TMi = CS.get("TMi", [128, 16, 128], BF16)
        TDr = CS.get("TDr", [128, 16, 128], BF16)
        TDi = CS.get("TDi", [128, 16, 128], BF16)
        A1c = CS.get("A1c", [128, 16, 2, 2], F32)
        ALc = CS.get("ALc", [128, 16, 2, 2], F32)
        ONES = CS.get("ONES", [128, 1024], BF16)
        carry = CS.get("carry", [128, 16, 2], F32)
        P4 = CS.get("P4", [128, 16, 2, 2], F32)
        seq_off = CS.off
        SEQ = CS.get("SEQ", [128, 16, 2, 128], F32)
        W16 = CS.get("W16", [128, 16, 2, 128], BF16)
        pw_off = CS.off
        Xb = CS.get("Xb", [128, 16, 2, 128], BF16)
        T1m = CS.get("T1m", [128, 4, 2, 128], BF16)
        T2m = CS.get("T2m", [128, 4, 2, 128], BF16)
        T1d = CS.get("T1d", [128, 4, 2, 128], BF16)
        T2d = CS.get("T2d", [128, 4, 2, 128], BF16)
        ts_off = CS.off
        tS = CS.get("tS", [128, 512], F32)
        sqS = CS.get("sqS", [128, 512], BF16)
        inS = CS.get("inS", [128, 512], BF16)
        sgS = CS.get("sgS", [128, 512], BF16)
        uT = [CS.get("uT%d" % i, [128, 4, 128], BF16) for i in range(2)]
        dsu = [CS.get("dsu%d" % i, [128, 4, 128], BF16) for i in range(2)]
        tab_end = CS.off
        yT = CS.get("yT", [128, 4, 128], BF16)
        s1 = CS.get("s1", [128, 4, 128], BF16)
        s2 = CS.get("s2", [128, 512], BF16)
        m1 = CS.get("m1", [128, 512], BF16)
        m2 = CS.get("m2", [128, 512], BF16)
        ysc = [CS.get("ysc%d" % i, [128, 4, 128], BF16) for i in range(2)]
        pct_off = CS.off
        pCt = CS.get("pCt", [128, 512], F32)
        CS5 = Carver()
        CS5.off = pct_off
        Bq16 = CS5.get("Bq16", [128, 4, 2, 128], BF16)
        NPS = 24
        CS2 = Carver()
        CS2.off = seq_off
        pscr = [CS2.get("pscr%d" % i, [128, 256], F32) for i in range(NPS)]
        assert CS2.off <= pw_off
        CS3 = Carver()
        CS3.off = pw_off
        PWr = CS3.get("PWr", [128, 16, 128], F32)
        PWi = CS3.get("PWi", [128, 16, 128], F32)
        assert CS3.off <= ts_off
        CS4 = Carver()
        CS4.off = ts_off
        tA = CS4.get("tA", [128, 16, 64], F32)
        tB = CS4.get("tB", [128, 16, 64], F32)
        assert CS4.off <= tab_end
        XS_NAMES = [("xs", i) for i in range(NXS)]
        S_ALL = (["w_u", "w_zs", "w_glu", "BbT", "BbTpad", "CmS", "TMr", "TMi", "TDr", "TDi", "A1c", "ALc", "ONES", "carry", "P4",
                  "pCt", "PWr", "PWi", "tA", "tB"] + [("pscr", i) for i in range(NPS)])

        load_w(w_u, w_in[:, 5120:5632], "w_u", "w_u")
        load_w(w_zs, w_in[:, 5632:6144], "w_zs", "w_zs")
        load_w(w_glu, w_glu_d, "w_glu", "w_glu")

        scr_i = [0]

        def new_scr(n):
            i = scr_i[0]
            scr_i[0] += 1
            assert i < NPS
            return pscr[i][:, 0:n], ("pscr", i)

        def tt(out, a, b, op):
            P.dve(lambda e: e.tensor_tensor(out=out[0], in0=a[0], in1=b[0], op=op),
                  reads=[a[1], b[1]], writes=[out[1]])

        def ts(out, a, s1_, s2_, op0, op1=None):
            if op1 is None:
                P.dve(lambda e: e.tensor_scalar(out=out[0], in0=a[0], scalar1=s1_, scalar2=None, op0=op0),
                      reads=[a[1]], writes=[out[1]])
            else:
                P.dve(lambda e: e.tensor_scalar(out=out[0], in0=a[0], scalar1=s1_, scalar2=s2_, op0=op0, op1=op1),
                      reads=[a[1]], writes=[out[1]])

        def exp_poly(xin, n, nterms, nsq):
            y = new_scr(n)
            ts(y, xin, 1.0 / (1 << nsq), None, ALU.mult)
            p = new_scr(n)
            t = new_scr(n)
            ts(p, y, 1.0 / nterms, 1.0, ALU.mult, ALU.add)
            for k in range(nterms - 1, 0, -1):
                tt(t, p, y, ALU.mult)
                ts(p, t, 1.0 / k, 1.0, ALU.mult, ALU.add)
            for _ in range(nsq):
                tt(t, p, p, ALU.mult)
                p, t = t, p
            return p

        hpi = sb("hpi", [128, 1], F32)
        P.pool(lambda e: e.memset(hpi[:], math.pi / 2), writes=["hpi"])

        def cplx_a(lr, li, ldt, n, want_coef, want_inv):
            dt = exp_poly(ldt, n, 12, 3)
            xr = new_scr(n)
            th = new_scr(n)
            tt(xr, lr, dt, ALU.mult)
            tt(th, li, dt, ALU.mult)
            mag = exp_poly(xr, n, 7, 0)
            s = new_scr(n)
            c = new_scr(n)
            P.act(lambda e: e.activation(out=s[0], in_=th[0], func=AF.Sin, scale=0.125), reads=[th[1]], writes=[s[1]])
            P.act(lambda e: e.activation(out=c[0], in_=th[0], func=AF.Sin, scale=-0.125, bias=hpi[:, 0:1]),
                  reads=[th[1], "hpi"], writes=[c[1]])
            t1 = new_scr(n)
            t2 = new_scr(n)
            for _ in range(3):
                tt(t1, s, c, ALU.mult)
                tt(t2, s, s, ALU.mult)
                ts(s, t1, 2.0, None, ALU.mult)
                ts(c, t2, -2.0, 1.0, ALU.mult, ALU.add)
            ar = new_scr(n)
            ai = new_scr(n)
            tt(ar, mag, c, ALU.mult)
            tt(ai, mag, s, ALU.mult)
            res = {"ar": ar, "ai": ai}
            if want_inv:
                ts(t1, xr, -1.0, None, ALU.mult)
                minv = exp_poly(t1, n, 7, 0)
                ir = new_scr(n)
                ii = new_scr(n)
                tt(ir, minv, c, ALU.mult)
                tt(ii, minv, s, ALU.mult)
                ts(ii, ii, -1.0, None, ALU.mult)
                res["ir"], res["ii"] = ir, ii
            if want_coef:
                nr = t1
                ts(nr, ar, -1.0, None, ALU.add)
                den = t2
                tt(den, lr, lr, ALU.mult)
                tt(th, li, li, ALU.mult)
                tt(den, den, th, ALU.add)
                P.dve(lambda e: e.reciprocal(out=den[0], in_=den[0]), reads=[den[1]], writes=[den[1]])
                cr = new_scr(n)
                ci = new_scr(n)
                tt(cr, nr, lr, ALU.mult)
                tt(th, ai, li, ALU.mult)
                tt(cr, cr, th, ALU.add)
                tt(cr, cr, den, ALU.mult)
                tt(ci, ai, lr, ALU.mult)
                tt(th, nr, li, ALU.mult)
                tt(ci, ci, th, ALU.subtract)
                tt(ci, ci, den, ALU.mult)
                res["cr"], res["ci"] = cr, ci
            return res

        def load_scr(src, n):
            t = new_scr(n)
            P.dma("sp", t[0], src, t[1], writes=[t[1]])
            return t

        lrB = load_scr(pB_lr, 256)
        liB = load_scr(pB_li, 256)
        ldtB = load_scr(pB_ldt, 256)
        brB = load_scr(pB_br, 256)
        biB = load_scr(pB_bi, 256)
        rB = cplx_a(lrB, liB, ldtB, 256, True, False)
        crB, ciB = rB["cr"], rB["ci"]
        t1 = lrB
        t2 = liB
        BbT_re = (BbT[:, :, 0:64], "BbT")
        BbT_im = (BbT[:, :, 64:128], "BbT")
        v3 = lambda a: (a[0].rearrange("p (k q) -> p k q", k=4), a[1])
        tt(t1, crB, brB, ALU.mult)
        tt(t2, ciB, biB, ALU.mult)
        tt(BbT_re, v3(t1), v3(t2), ALU.subtract)
        tt(t1, crB, biB, ALU.mult)
        tt(t2, ciB, brB, ALU.mult)
        tt(BbT_im, v3(t1), v3(t2), ALU.add)
        for gl in range(8):
            P.dve(lambda e, gl=gl: e.tensor_scalar(out=BbTpad[:, gl:32:8, :], in0=BbT[:, :, :], scalar1=rowmask[:, gl:gl + 1],
                                                   scalar2=None, op0=ALU.mult), reads=["BbT", "rowmask"], writes=["BbTpad"])
        scr_i[0] = 0
        lrS = load_scr(pS_lr, 16)
        liS = load_scr(pS_li, 16)
        ldtS = load_scr(pS_ldt, 16)
        rS = cplx_a(lrS, liS, ldtS, 16, False, True)

        def cat_table(dst, dname, re, im):
            P.dve(lambda e: e.tensor_copy(out=dst[:, :, 0, 0], in_=re[0]), reads=[re[1]], writes=[dname])
            P.dve(lambda e: e.tensor_copy(out=dst[:, :, 0, 1], in_=im[0]), reads=[im[1]], writes=[dname])
            P.dve(lambda e: e.tensor_scalar(out=dst[:, :, 1, 0], in0=im[0], scalar1=-1.0, scalar2=None, op0=ALU.mult),
                  reads=[im[1]], writes=[dname])
            P.dve(lambda e: e.tensor_copy(out=dst[:, :, 1, 1], in_=re[0]), reads=[re[1]], writes=[dname])

        cat_table(A1c, "A1c", rS["ar"], rS["ai"])

        def power_table(br, bi, dr, di, last=None):
            P.dve(lambda e: e.memset(PWr[:, :, 0:1], 1.0), writes=["PWr"])
            P.dve(lambda e: e.memset(PWi[:, :, 0:1], 0.0), writes=["PWi"])
            P.dve(lambda e: e.tensor_copy(out=PWr[:, :, 1], in_=br[0]), reads=[br[1]], writes=["PWr"])
            P.dve(lambda e: e.tensor_copy(out=PWi[:, :, 1], in_=bi[0]), reads=[bi[1]], writes=["PWi"])
            k = 1
            while k < 128:
                n = min(k, 127 - k)
                if n > 0:
                    sr, si = PWr[:, :, 1:1 + n], PWi[:, :, 1:1 + n]
                    orr, oi = PWr[:, :, k + 1:k + 1 + n], PWi[:, :, k + 1:k + 1 + n]
                    cr = PWr[:, :, k:k + 1].to_broadcast([128, 16, n])
                    ci = PWi[:, :, k:k + 1].to_broadcast([128, 16, n])
                    a_, b_ = tA[:, :, 0:n], tB[:, :, 0:n]
                    rw = ["PWr", "PWi"]
                    P.dve(lambda e, sr=sr, cr=cr, a_=a_: e.tensor_tensor(out=a_, in0=sr, in1=cr, op=ALU.mult), reads=rw, writes=["tA"])
                    P.dve(lambda e, si=si, ci=ci, b_=b_: e.tensor_tensor(out=b_, in0=si, in1=ci, op=ALU.mult), reads=rw, writes=["tB"])
                    P.dve(lambda e, orr=orr, a_=a_, b_=b_: e.tensor_tensor(out=orr, in0=a_, in1=b_, op=ALU.subtract),
                          reads=["tA", "tB", "PWr"], writes=["PWr"])
                    P.dve(lambda e, sr=sr, ci=ci, a_=a_: e.tensor_tensor(out=a_, in0=sr, in1=ci, op=ALU.mult), reads=rw, writes=["tA"])
                    P.dve(lambda e, si=si, cr=cr, b_=b_: e.tensor_tensor(out=b_, in0=si, in1=cr, op=ALU.mult), reads=rw, writes=["tB"])
                    P.dve(lambda e, oi=oi, a_=a_, b_=b_: e.tensor_tensor(out=oi, in0=a_, in1=b_, op=ALU.add),
                          reads=["tA", "tB", "PWi"], writes=["PWi"])
                k *= 2
            if last is not None:
                cat_table(last, "ALc", (PWr[:, :, 127], "PWr"), (PWi[:, :, 127], "PWi"))
            P.act(lambda e: e.copy(out=dr, in_=PWr), reads=["PWr"], writes=[dname_of[id(dr)]])
            P.act(lambda e: e.copy(out=di, in_=PWi), reads=["PWi"], writes=[dname_of[id(di)]])

        dname_of = {id(TMr): "TMr", id(TMi): "TMi", id(TDr): "TDr", id(TDi): "TDi"}
        power_table(rS["ir"], rS["ii"], TMr, TMi)
        power_table(rS["ar"], rS["ai"], TDr, TDi, last=ALc)
        P.dma("sp", pCt, pC_c, "pCt", writes=["pCt"])
        P.pool(lambda e: e.memset(CmS, 0.0), writes=["CmS"])
        pC4 = pCt.rearrange("p (g r c) -> p g r c", g=16, r=2)
        for ri, sgn in ((0, 1.0), (1, -1.0)):
            for par, c0 in ((0, 0), (1, 32)):
                P.dve(lambda e, ri=ri, sgn=sgn, par=par, c0=c0: e.tensor_scalar(
                    out=CmS[:, :, ri, c0:c0 + 16], in0=pC4[:, par:16:2, ri, :], scalar1=sgn, scalar2=None, op0=ALU.mult),
                    reads=["pCt", "CmS"], writes=["CmS"])
        P.pool(lambda e: e.memset(carry, 0.0), writes=["carry"])
        P.pool(lambda e: e.memset(ONES, 1.0), writes=["ONES"])
        P.pool(lambda e: e.memset(ONES[:, 0:1024:128], 0.0), reads=["ONES"], writes=["ONES"])
        SEG_NAMES = ["T1m", "T2m", "T1d", "T2d", "tS", "sqS", "inS", "sgS"] + [(n, q) for n in ("SEQ", "W16", "Xb") for q in range(4)] + [(n, i) for n in ("uT", "dsu") for i in range(2)]
        P.alias(SEG_NAMES, [("pscr", i) for i in range(NPS)] + ["PWr", "PWi", "tA", "tB"])
        P.alias(["Bq16"], ["pCt"])

        T = 128
        NSEG = L // T if 'S' in STAGES else 0

        def seg_F1(sg):
            t0 = sg * T
            db = sg % 2
            xres_seg = [("xT", sg)]

            def fu(e, t0=t0):
                ins = None
                for k in range(4):
                    for d_ in range(8):
                        ins = e.matmul(pb[0][:, k * 128:(k + 1) * 128], lhsT=w_u[:, d_, k * 128:(k + 1) * 128],
                                       rhs=xT[:, d_, t0:t0 + T], start=(d_ == 0), stop=(d_ == 7))
                return ins
            P.pe(fu, reads=["w_u"] + xres_seg, writes=[PS(0)])
            p0v = pb[0][:, :].rearrange("p (a b) -> p a b", a=4)
            P.act(lambda e, db=db: e.copy(out=uT[db], in_=p0v), reads=[PS(0)], writes=[("uT", db)])
            for k in range(4):
                P.act(lambda e, db=db, k=k: e.activation(out=dsu[db][:, k, :], in_=pb[0][:, k * 128:(k + 1) * 128], func=AF.Copy, scale=dsk[:, k:k + 1]),
                      reads=[PS(0), "dsk"], writes=[("dsu", db)])
            for qd in range(4):
                qb = 1 + 2 * (qd % 2)
                Q = pbig[:, 512 * qb:512 * qb + 1024]
                Q4 = Q.rearrange("p (a b c) -> p a b c", a=4, b=2)

                def fbu(e, qd=qd, Q=Q, db=db):
                    ins = None
                    for gi in range(4):
                        for gh in range(2):
                            g = 16 * gh + 4 * qd + gi
                            k = g // 8
                            for ri in range(2):
                                c0 = (gi * 2 + ri) * 128
                                ins = e.matmul(Q[64 * gh:64 * gh + 64, c0:c0 + 128], lhsT=BbTpad[:, g, 64 * ri:64 * ri + 64],
                                               rhs=uT[db][:, k, :], start=True, stop=True)
                    return ins
                P.pe(fbu, reads=["BbTpad", ("uT", db)], writes=[PS(qb), PS(qb + 1)])
                g4 = slice(4 * qd, 4 * qd + 4)
                tmr = TMr[:, g4, :].unsqueeze(2).to_broadcast([128, 4, 2, 128])
                tmi = TMi[:, g4, :].unsqueeze(2).to_broadcast([128, 4, 2, 128])
                rdq = [PS(qb), PS(qb + 1)]
                P.act(lambda e, Q4=Q4: e.copy(out=Bq16, in_=Q4), reads=rdq, writes=["Bq16"])
                P.dve(lambda e, tmr=tmr: e.tensor_tensor(out=T1m, in0=Bq16, in1=tmr, op=ALU.mult), reads=["Bq16", "TMr"], writes=["T1m"])
                P.dve(lambda e, tmi=tmi: e.tensor_tensor(out=T2m, in0=Bq16, in1=tmi, op=ALU.mult), reads=["Bq16", "TMi"], writes=["T2m"])
                P.dve(lambda e, g4=g4: e.tensor_tensor(out=SEQ[:, g4, 0, :], in0=T1m[:, :, 0, :], in1=T2m[:, :, 1, :], op=ALU.subtract),
                       reads=["T1m", "T2m"], writes=[("SEQ", qd)])
                P.dve(lambda e, g4=g4: e.tensor_tensor(out=SEQ[:, g4, 1, :], in0=T1m[:, :, 1, :], in1=T2m[:, :, 0, :], op=ALU.add),
                       reads=["T1m", "T2m"], writes=[("SEQ", qd)])

        def seg_F2(sg):
            t0 = sg * T
            db = sg % 2
            xres_seg = [("xT", sg)]
            cb = carry.unsqueeze(3).to_broadcast([128, 16, 2, 2])
            P.dve(lambda e, cb=cb: e.tensor_tensor(out=P4, in0=A1c, in1=cb, op=ALU.mult), reads=["A1c", "carry"], writes=["P4"])
            P.dve(lambda e: e.tensor_tensor(out=SEQ[:, :, :, 0], in0=SEQ[:, :, :, 0], in1=P4[:, :, 0, :], op=ALU.add),
                  reads=["P4"] + [("SEQ", q) for q in range(4)], writes=[("SEQ", q) for q in range(4)])
            P.dve(lambda e: e.tensor_tensor(out=SEQ[:, :, :, 0], in0=SEQ[:, :, :, 0], in1=P4[:, :, 1, :], op=ALU.add),
                  reads=["P4"] + [("SEQ", q) for q in range(4)], writes=[("SEQ", q) for q in range(4)])
            for qd in range(4):
                sq = SEQ[:, 4 * qd:4 * qd + 4, :, :].rearrange("p a b c -> p (a b c)")
                P.dve(lambda e, sq=sq: e.tensor_tensor_scan(out=sq, data0=ONES, data1=sq, initial=0.0, op0=ALU.mult, op1=ALU.add),
                      reads=[("SEQ", qd), "ONES"], writes=[("SEQ", qd)])
            lb = SEQ[:, :, :, T - 1].unsqueeze(3).to_broadcast([128, 16, 2, 2])
            P.dve(lambda e, lb=lb: e.tensor_tensor(out=P4, in0=ALc, in1=lb, op=ALU.mult), reads=["ALc"] + [("SEQ", q) for q in range(4)], writes=["P4"])
            P.dve(lambda e: e.tensor_tensor(out=carry, in0=P4[:, :, 0, :], in1=P4[:, :, 1, :], op=ALU.add), reads=["P4"], writes=["carry"])
            for qd in range(4):
                g4 = slice(4 * qd, 4 * qd + 4)
                P.act(lambda e, g4=g4: e.copy(out=W16[:, g4, :, :].rearrange("p a b c -> p (a b c)"),
                                              in_=SEQ[:, g4, :, :].rearrange("p a b c -> p (a b c)")),
                      reads=[("SEQ", qd)], writes=[("W16", qd)])

        def seg_B1(sg):
            t0 = sg * T
            db = sg % 2
            xres_seg = [("xT", sg)]
            for qd in range(4):
                g4 = slice(4 * qd, 4 * qd + 4)
                tdr = TDr[:, g4, :].unsqueeze(2).to_broadcast([128, 4, 2, 128])
                tdi = TDi[:, g4, :].unsqueeze(2).to_broadcast([128, 4, 2, 128])
                W4 = W16[:, g4, :, :]
                P.dve(lambda e, W4=W4, tdr=tdr: e.tensor_tensor(out=T1d, in0=W4, in1=tdr, op=ALU.mult), reads=[("W16", qd), "TDr"], writes=["T1d"])
                P.dve(lambda e, W4=W4, tdi=tdi: e.tensor_tensor(out=T2d, in0=W4, in1=tdi, op=ALU.mult), reads=[("W16", qd), "TDi"], writes=["T2d"])
                P.dve(lambda e, g4=g4: e.tensor_tensor(out=Xb[:, g4, 0, :], in0=T1d[:, :, 0, :], in1=T2d[:, :, 1, :], op=ALU.subtract),
                       reads=["T1d", "T2d"], writes=[("Xb", qd)])
                P.dve(lambda e, g4=g4: e.tensor_tensor(out=Xb[:, g4, 1, :], in0=T1d[:, :, 1, :], in1=T2d[:, :, 0, :], op=ALU.add),
                       reads=["T1d", "T2d"], writes=[("Xb", qd)])

        def seg_B2a(sg):
            t0 = sg * T
            db = sg % 2
            xres_seg = [("xT", sg)]
            def fzs(e, t0=t0):
                ins = None
                for ee in range(4):
                    for d_ in range(8):
                        ins = e.matmul(pb[0][:, ee * 128:(ee + 1) * 128], lhsT=w_zs[:, d_, ee * 128:(ee + 1) * 128],
                                       rhs=xT[:, d_, t0:t0 + T], start=(d_ == 0), stop=(d_ == 7))
                return ins
            P.pe(fzs, reads=["w_zs"] + xres_seg, writes=[PS(0)])
            P.act(lambda e: e.activation(out=s2, in_=pb[0][:, :], func=AF.Sigmoid), reads=[PS(0)], writes=["s2"])
            P.dve(lambda e: e.tensor_tensor(out=m2, in0=pb[0][:, :], in1=s2, op=ALU.mult), reads=[PS(0), "s2"], writes=["m2"])
            for gh in range(2):
                ybank = 5 + gh

                def fy(e, gh=gh, ybank=ybank):
                    ins = None
                    pr0 = 64 * gh
                    for pl in range(8):
                        pr = 8 * gh + pl
                        rows = 32 * (pr % 4)
                        c0 = ((pr // 4) % 2) * 128
                        tp = (64 * gh, rows)
                        o32 = pb[ybank][rows:rows + 32, c0:c0 + 128]
                        o16 = pb[ybank][rows:rows + 16, c0:c0 + 128]
                        e.matmul(o32, lhsT=CmS[pr0:pr0 + 64, pl, 0, 16:48], rhs=Xb[pr0:pr0 + 64, 2 * pl + 1, 0, :], start=True, stop=False, tile_position=tp)
                        e.matmul(o32, lhsT=CmS[pr0:pr0 + 64, pl, 1, 16:48], rhs=Xb[pr0:pr0 + 64, 2 * pl + 1, 1, :], start=False, stop=False, tile_position=tp)
                        e.matmul(o16, lhsT=CmS[pr0:pr0 + 64, pl, 0, 0:16], rhs=Xb[pr0:pr0 + 64, 2 * pl, 0, :], start=False, stop=False, tile_position=tp)
                        ins = e.matmul(o16, lhsT=CmS[pr0:pr0 + 64, pl, 1, 0:16], rhs=Xb[pr0:pr0 + 64, 2 * pl, 1, :], start=False, stop=True, tile_position=tp)
                    return ins
                P.pe(fy, reads=["CmS"] + [("Xb", q) for q in range(4)], writes=[PS(ybank)])
                dsh = dsu[db][:, 2 * gh:2 * gh + 2, :].rearrange("p a b -> p (a b)")
                P.dve(lambda e, gh=gh, ybank=ybank, dsh=dsh: e.tensor_tensor(out=tS[:, 256 * gh:256 * gh + 256], in0=pb[ybank][:, 0:256], in1=dsh, op=ALU.add),
                      reads=[PS(ybank), ("dsu", db)], writes=["tS"])
            P.act(lambda e: e.activation(out=sqS, in_=tS, func=AF.Square), reads=["tS"], writes=["sqS"])
            P.pool(lambda e: e.tensor_scalar(out=inS, in0=sqS, scalar1=0.044715, scalar2=1.0, op0=ALU.mult, op1=ALU.add),
                   reads=["sqS"], writes=["inS"])
            P.pool(lambda e: e.tensor_tensor(out=inS, in0=inS, in1=tS, op=ALU.mult), reads=["inS", "tS"], writes=["inS"])
            P.act(lambda e: e.activation(out=sgS, in_=inS, func=AF.Sigmoid, scale=1.5957691216), reads=["inS"], writes=["sgS"])
            yTf = yT.rearrange("p a b -> p (a b)")
            P.pool(lambda e: e.tensor_tensor(out=yTf, in0=tS, in1=sgS, op=ALU.mult), reads=["tS", "sgS"], writes=["yT"])


        def seg_B2b(sg):
            t0 = sg * T
            db = sg % 2
            xres_seg = [("xT", sg)]
            yTf = yT.rearrange("p a b -> p (a b)")
            def fglu(e):
                ins = None
                for ee in range(4):
                    for c_ in range(4):
                        ins = e.matmul(pb[7][:, ee * 128:(ee + 1) * 128], lhsT=w_glu[:, c_, ee * 128:(ee + 1) * 128],
                                       rhs=yT[:, c_, :], start=(c_ == 0), stop=(c_ == 3))
                return ins
            P.pe(fglu, reads=["w_glu", "yT"], writes=[PS(7)])

            for ee in range(4):
                P.act(lambda e, ee=ee: e.activation(out=s1[:, ee, :], in_=pb[7][:, ee * 128:(ee + 1) * 128], func=AF.Sigmoid,
                                                    bias=bglu[:, ee:ee + 1]), reads=[PS(7), "bglu"], writes=["s1"])
            s1f = s1.rearrange("p a b -> p (a b)")
            P.pool(lambda e: e.tensor_tensor(out=m1, in0=yTf, in1=s1f, op=ALU.mult), reads=["yT", "s1"], writes=["m1"])
            yscf = ysc[db].rearrange("p a b -> p (a b)")
            P.pool(lambda e, yscf=yscf: e.tensor_tensor(out=yscf, in0=m1, in1=m2, op=ALU.mult), reads=["m1", "m2"], writes=[("ysc", db)])
            P.dma("sp", ys_d[:, :, t0:t0 + T], ysc[db], ("ysc", db), reads=[("ysc", db)], writes=[("ys_d", sg)])


        if NSEG:
            seg_F1(0)
            seg_F2(0)
        for sg in range(NSEG):
            if sg + 1 < NSEG:
                seg_F1(sg + 1)
            if sg >= 1:
                seg_B2b(sg - 1)
            seg_B1(sg)
            if sg + 1 < NSEG:
                seg_F2(sg + 1)
            seg_B2a(sg)
        if NSEG:
            seg_B2b(NSEG - 1)
        S_NAMES = (S_ALL + SEG_NAMES + ["Bq16", "yT", "s1", "s2", "m1", "m2"]
                   + [("ysc", i) for i in range(2)])

        CA = Carver()
        qT = CA.get("qT", [128, L], BF16)
        kT = CA.get("kT", [128, L], BF16)
        vaug = CA.get("vaug", [128, 32, 2, 128], BF16)
        acc = [CA.get("acc%d" % i, [128, L], F32) for i in range(2)]
        wq2 = [CA.get("wq%d" % i, [128, 8, 128], BF16) for i in range(2)]
        wk2 = [CA.get("wk%d" % i, [128, 8, 128], BF16) for i in range(2)]
        wv2 = [CA.get("wv%d" % i, [128, 8, 128], BF16) for i in range(2)]
        mk2 = [CA.get("mk%d" % i, [128, 2, 256], BF16) for i in range(2)]
        wza = CA.get("wza", [128, 8, 128], BF16)
        Pe = [CA.get("Pe%d" % i, [128, 256], BF16) for i in range(2)]
        Pm = [CA.get("Pm%d" % i, [128, 256], BF16) for i in range(6)]
        sgz = CA.get("sgz", [128, 512], F32)
        szt = CA.get("szt", [128, 512], F32)
        rLt = CA.get("rLt", [128, 512], F32)
        o1t = CA.get("o1t", [128, 512], F32)
        A_NAMES = (["qT", "kT", "vaug", "vones", "acc0", "acc1", "wza", "sgz", "szt", "rLt", "o1t"]
                   + [("Pe", i) for i in range(2)] + [("Pm", i) for i in range(6)]
                   + [(n, i) for n in ("wq", "wk", "wv", "mk") for i in range(2)])
        P.alias(A_NAMES, S_NAMES)
        P.alias(["attnT"], XS_NAMES)
        P.pool(lambda e: e.memset(vaug[:, :, :, 64:128], 1.0), writes=["vones"])
        rot = {"p": 0, "s": 0, "o": 0, "e": 0, "m": 0}
        for hp in range(4 if 'A' in STAGES else 0):
            for g in range(3):
                r = DIL[g]
                nb = 32 // r
                ws = (3 * hp + g) % 2
                wq, wk, wv, mk = wq2[ws], wk2[ws], wv2[ws], mk2[ws]
                wqn, wkn, wvn, mkn = ("wq", ws), ("wk", ws), ("wv", ws), ("mk", ws)
                load_w(wq, w_in[:, g * 512 + hp * 128: g * 512 + hp * 128 + 128], wqn, wqn)
                load_w(wk, w_in[:, 1536 + g * 512 + hp * 128: 1536 + g * 512 + hp * 128 + 128], wkn, wkn)
                load_w(wv, w_in[:, 3072 + g * 512 + hp * 128: 3072 + g * 512 + hp * 128 + 128], wvn, wvn)
                P.dma("pool", mk, mask_d[g * 8 + 2 * hp: g * 8 + 2 * hp + 2].rearrange("h p q -> p h q"), mkn, writes=[mkn])
                for (wt, wres, dst, dres) in ((wq, wqn, qT, "qT"), (wk, wkn, kT, "kT")):
                    for c in range(8):
                        bank = rot["p"] % 2
                        rot["p"] += 1

                        def fqk(e, wt=wt, c=c, bank=bank):
                            ins = None
                            for d_ in range(8):
                                ins = e.matmul(pb[bank][:, :], lhsT=wt[:, d_, :], rhs=xT[:, d_, c * 512:(c + 1) * 512],
                                               start=(d_ == 0), stop=(d_ == 7))
                            return ins
                        P.pe(fqk, reads=[wres] + XT_ALL[4 * c:4 * c + 4], writes=[PS(bank)])
                        P.act(lambda e, dst=dst, c=c, bank=bank: e.copy(out=dst[:, c * 512:(c + 1) * 512], in_=pb[bank][:, :]),
                              reads=[PS(bank)], writes=[dres])
                for vq in range(8):
                    bank = rot["p"] % 2
                    rot["p"] += 1

                    def fv(e, vq=vq, bank=bank, r=r, nb=nb, wv=wv):
                        ins = None
                        for j in range(4):
                            tile = 4 * vq + j
                            ph, jb = divmod(tile, nb)
                            base = 128 * jb * r + ph
                            for d_ in range(8):
                                ins = e.matmul(pb[bank][:, j * 128:(j + 1) * 128], lhsT=xT[:, d_, base:base + 127 * r + 1:r],
                                               rhs=wv[:, d_, :], start=(d_ == 0), stop=(d_ == 7))
                        return ins
                    P.pe(fv, reads=[wvn] + XT_ALL, writes=[PS(bank)])
                    for hh in range(2):
                        srcv = pb[bank][:, :].rearrange("p (a h b) -> p a h b", a=4, h=2)[:, :, hh, :]
                        if vq % 2 == 0:
                            P.act(lambda e, vq=vq, srcv=srcv, hh=hh: e.copy(out=vaug[:, 4 * vq:4 * vq + 4, hh, 0:64], in_=srcv),
                                  reads=[PS(bank)], writes=["vaug"])
                        else:
                            P.dve(lambda e, vq=vq, srcv=srcv, hh=hh: e.tensor_copy(out=vaug[:, 4 * vq:4 * vq + 4, hh, 0:64], in_=srcv),
                                  reads=[PS(bank)], writes=["vaug"])
                for hh in range(2):
                    hr = 64 * hh
                    accn = "acc%d" % hh
                    units = [(ph, jb) for ph in range(r) for jb in range(nb)]
                    LA = 4
                    pm_of = {}
                    st8 = {"obank": None, "qf": None}

                    def emit_s(ui, hh=hh, hr=hr):
                        ph, jb = units[ui]
                        base = 128 * jb * r + ph
                        nq = 256 if jb < nb - 1 else 128
                        sbank = (2, 3, 6, 7)[rot["s"] % 4]
                        rot["s"] += 1
                        pe_i = rot["e"] % 2
                        rot["e"] += 1
                        pm_i = rot["m"] % 6
                        rot["m"] += 1
                        pm_of[ui] = pm_i

                        def fs(e, base=base, nq=nq, sbank=sbank, r=r, hr=hr):
                            return e.matmul(pb[sbank][:, 0:nq], lhsT=kT[hr:hr + 64, base:base + 127 * r + 1:r],
                                            rhs=qT[hr:hr + 64, base:base + (nq - 1) * r + 1:r], start=True, stop=True)
                        P.pe(fs, reads=["qT", "kT"], writes=[PS(sbank)])
                        P.act(lambda e, pe_i=pe_i, nq=nq, sbank=sbank: e.activation(
                            out=Pe[pe_i][:, 0:nq], in_=pb[sbank][:, 0:nq], func=AF.Exp, scale=0.125),
                            reads=[PS(sbank)], writes=[("Pe", pe_i)])
                        P.dve(lambda e, pe_i=pe_i, pm_i=pm_i, nq=nq, mk=mk: e.tensor_tensor(
                            out=Pm[pm_i][:, 0:nq], in0=Pe[pe_i][:, 0:nq], in1=mk[:, hh, 0:nq], op=ALU.mult),
                            reads=[("Pe", pe_i), mkn], writes=[("Pm", pm_i)])

                    def emit_pv(ui, hh=hh, accn=accn):
                        ph, jb = units[ui]
                        oq = ui % 4
                        if oq == 0:
                            st8["obank"] = 4 + rot["o"] % 2
                            rot["o"] += 1
                            st8["qf"] = (ph, jb)
                        obank = st8["obank"]
                        tile = ph * nb + jb
                        pm_i = pm_of[ui]
                        prev_pm = pm_of.get(ui - 1)

                        def fpv(e, obank=obank, oq=oq, tile=tile, pm_i=pm_i, jb=jb, prev_pm=prev_pm):
                            ins = e.matmul(pb[obank][:, oq * 128:(oq + 1) * 128], lhsT=vaug[:, tile, hh, :],
                                           rhs=Pm[pm_i][:, 0:128], start=True, stop=(jb == 0))
                            if jb > 0:
                                ins = e.matmul(pb[obank][:, oq * 128:(oq + 1) * 128], lhsT=vaug[:, tile - 1, hh, :],
                                               rhs=Pm[prev_pm][:, 128:256], start=False, stop=True)
                            return ins
                        rd = ["vaug", "vones", ("Pm", pm_i)] + ([("Pm", prev_pm)] if jb > 0 else [])
                        P.pe(fpv, reads=rd, writes=[PS(obank)])
                        if oq == 3:
                            ph0, jb0 = st8["qf"]
                            a = acc[hh]
                            if r == 1:
                                av = a[:, 128 * jb0:128 * jb0 + 512]
                                sv = pb[obank][:, :]
                            elif r == 4:
                                av = a[:, :].rearrange("p (j i f) -> p f j i", i=128, f=4)[:, ph0, jb0:jb0 + 4, :]
                                sv = pb[obank][:, :].rearrange("p (j i) -> p j i", j=4)
                            else:
                                av = a[:, :].rearrange("p (j i f) -> p f j i", j=2, f=16)[:, ph0:ph0 + 2, :, :]
                                sv = pb[obank][:, :].rearrange("p (f j i) -> p f j i", f=2, j=2)
                            if g == 0:
                                P.dve(lambda e, av=av, sv=sv: e.tensor_copy(out=av, in_=sv), reads=[PS(obank)], writes=[accn])
                            else:
                                P.dve(lambda e, av=av, sv=sv: e.tensor_tensor(out=av, in0=sv, in1=av, op=ALU.add),
                                      reads=[PS(obank), accn], writes=[accn])

                    for i in range(len(units) + LA):
                        if i < len(units):
                            emit_s(i)
                        if i >= LA:
                            emit_pv(i - LA)
            load_w(wza, w_in[:, 4608 + hp * 128:4608 + hp * 128 + 128], "wza", "wza")
            for c in range(8):
                bank = rot["p"] % 2
                rot["p"] += 1
                cs = slice(c * 512, (c + 1) * 512)

                def fza(e, c=c, bank=bank):
                    ins = None
                    for d_ in range(8):
                        ins = e.matmul(pb[bank][:, :], lhsT=wza[:, d_, :], rhs=xT[:, d_, c * 512:(c + 1) * 512],
                                       start=(d_ == 0), stop=(d_ == 7))
                    return ins
                P.pe(fza, reads=["wza"] + XT_ALL[4 * c:4 * c + 4], writes=[PS(bank)])
                P.act(lambda e, bank=bank: e.activation(out=sgz, in_=pb[bank][:, :], func=AF.Sigmoid), reads=[PS(bank)], writes=["sgz"])
                P.dve(lambda e, bank=bank: e.tensor_tensor(out=szt, in0=pb[bank][:, :], in1=sgz, op=ALU.mult),
                      reads=[PS(bank), "sgz"], writes=["szt"])
                P.dve(lambda e, cs=cs: e.tensor_copy(out=rLt[0:64, :], in_=acc[0][64:128, cs]), reads=["acc0"], writes=["rLt"])
                P.dve(lambda e, cs=cs: e.tensor_copy(out=rLt[64:128, :], in_=acc[1][64:128, cs]), reads=["acc1"], writes=["rLt"])
                P.dve(lambda e: e.reciprocal(out=rLt, in_=rLt), reads=["rLt"], writes=["rLt"])
                P.dve(lambda e, cs=cs: e.tensor_copy(out=o1t[0:64, :], in_=acc[0][0:64, cs]), reads=["acc0"], writes=["o1t"])
                P.dve(lambda e, cs=cs: e.tensor_copy(out=o1t[64:128, :], in_=acc[1][0:64, cs]), reads=["acc1"], writes=["o1t"])
                P.pool(lambda e: e.tensor_tensor(out=o1t, in0=o1t, in1=rLt, op=ALU.mult), reads=["o1t", "rLt"], writes=["o1t"])
                P.pool(lambda e, hp=hp, cs=cs: e.tensor_tensor(out=attnT[:, hp, cs], in0=o1t, in1=szt, op=ALU.mult),
                       reads=["o1t", "szt"], writes=["attnT"])
        if dbg:
            P.dma("sp", attn_dbg, attnT[:, :, :], "attn_dbg", reads=["attnT"], writes=["attn_dbg"])

        CM = Carver()
        w_au = CM.get("w_au", [128, 4, D], BF16)
        w_su = CM.get("w_su", [128, 4, D], BF16)
        w_o = CM.get("w_o", [128, 8, D], BF16)
        wga = CM.get("wga", [128, 8, D], BF16)
        wgs = CM.get("wgs", [128, 8, D], BF16)
        merged = CM.get("merged", [128, 8, 512], BF16)
        yscM = CM.get("yscM", [128, 4, 512], BF16)
        sgb = [CM.get("sgb%d" % i, [128, 512], BF16) for i in range(3)]
        mma = [CM.get("mma%d" % i, [128, 512], BF16) for i in range(2)]
        mmb = [CM.get("mmb%d" % i, [128, 512], BF16) for i in range(2)]
        xres = [CM.get("xres%d" % i, [128, D], F32) for i in range(2)]
        hbuf = [CM.get("hbuf%d" % i, [128, D], F32) for i in range(2)]
        lng = CM.get("lng", [128, D], F32)
        lnb = CM.get("lnb", [128, D], F32)
        M_NAMES = (["w_au", "w_su", "w_o", "wga", "wgs", "merged", "yscM", "lng", "lnb"]
                   + [(("hbuf", i), eh) for i in range(2) for eh in range(2)]
                   + [("stats", i, eh) for i in range(2) for eh in range(2)] + [(n, i) for n in ("mv", "rstd", "nb") for i in range(2)]
                   + [("sgb", i) for i in range(3)]
                   + [(n, i) for n in ("mma", "mmb", "xres") for i in range(2)])
        P.alias(M_NAMES, A_NAMES + S_NAMES)
        P.dma("sp", lng, lng_d, "lng", writes=["lng"])
        P.dma("sp", lnb, lnb_d, "lnb", writes=["lnb"])
        load_w(w_au, w_au_d, "w_au", "w_au")
        load_w(w_su, w_su_d, "w_su", "w_su")
        load_w(wga, w_in[:, 6144:7168], "wga", "wga")
        load_w(wgs, w_in[:, 7168:8192], "wgs", "wgs")
        load_w(w_o, w_o_d, "w_o", "w_o")
        stats2 = [CM.get("stats%d" % i, [128, 12], F32) for i in range(2)]
        mv2 = [CM.get("mv%d" % i, [128, 2], F32) for i in range(2)]
        rstd2 = [CM.get("rstd%d" % i, [128, 1], F32) for i in range(2)]
        ln_queue = []

        def ln_front(c, t4):
            tt_ = 4 * c + t4
            xi = tt_ % 2
            hb, hbn = hbuf[xi], ("hbuf", xi)
            st_, mv_, rs_ = stats2[xi], mv2[xi], rstd2[xi]
            if tt_ == 0:
                P.dma("sp", xres[0], x[0:128, :], ("xres", 0), writes=[("xres", 0)])
            if tt_ + 1 < 32:
                nx = (tt_ + 1) % 2
                P.dma("sp", xres[nx], x[(tt_ + 1) * 128:(tt_ + 2) * 128, :], ("xres", nx), writes=[("xres", nx)])
            banks = []
            for eh in range(2):
                bank = 6 + rm["b"] % 2
                rm["b"] += 1
                banks.append(bank)

                def fo(e, t4=t4, eh=eh, bank=bank):
                    ins = None
                    for dm in range(8):
                        ins = e.matmul(pb[bank][:, :], lhsT=merged[:, dm, t4 * 128:(t4 + 1) * 128], rhs=w_o[:, dm, eh * 512:(eh + 1) * 512],
                                       start=(dm == 0), stop=(dm == 7))
                    return ins
                P.pe(fo, reads=["merged", "w_o"], writes=[PS(bank)])
            for eh in range(2):
                bank = banks[eh]
                P.dve(lambda e, xi=xi, eh=eh, bank=bank, hb=hb: e.scalar_tensor_tensor(
                    out=hb[:, eh * 512:(eh + 1) * 512], in0=xres[xi][:, eh * 512:(eh + 1) * 512], scalar=ALPHA,
                    in1=pb[bank][:, :], op0=ALU.mult, op1=ALU.add), reads=[("xres", xi), PS(bank)], writes=[(hbn, eh)])
            for eh in range(2):
                P.dve(lambda e, eh=eh, hb=hb, st_=st_: e.bn_stats(out=st_[:, eh * 6:(eh + 1) * 6], in_=hb[:, eh * 512:(eh + 1) * 512]),
                      reads=[(hbn, eh)], writes=[("stats", xi, eh)])
            P.dve(lambda e, st_=st_, mv_=mv_: e.bn_aggr(out=mv_, in_=st_), reads=[("stats", xi, 0), ("stats", xi, 1)], writes=[("mv", xi)])
            P.dve(lambda e, mv_=mv_, rs_=rs_: e.tensor_scalar(out=rs_, in0=mv_[:, 1:2], scalar1=EPS, scalar2=None, op0=ALU.add),
                  reads=[("mv", xi)], writes=[("rstd", xi)])
            P.act(lambda e, rs_=rs_: e.activation(out=rs_, in_=rs_, func=AF.Sqrt), reads=[("rstd", xi)], writes=[("rstd", xi)])

        nb2 = [CM.get("nb%d" % i, [128, 1], F32) for i in range(2)]

        def ln_back(c, t4):
            tt_ = 4 * c + t4
            xi = tt_ % 2
            hb, hbn = hbuf[xi], ("hbuf", xi)
            mv_, rs_, nb_ = mv2[xi], rstd2[xi], nb2[xi]
            P.dve(lambda e, rs_=rs_: e.reciprocal(out=rs_, in_=rs_), reads=[("rstd", xi)], writes=[("rstd", xi)])
            P.dve(lambda e, mv_=mv_, rs_=rs_, nb_=nb_: e.tensor_scalar(out=nb_, in0=mv_[:, 0:1], scalar1=rs_[:, 0:1], scalar2=-1.0,
                                                                    op0=ALU.mult, op1=ALU.mult), reads=[("mv", xi), ("rstd", xi)], writes=[("nb", xi)])
            for eh in range(2):
                hs = hb[:, eh * 512:(eh + 1) * 512]
                hr = [(hbn, eh)]
                P.act(lambda e, hs=hs, rs_=rs_, nb_=nb_: e.activation(out=hs, in_=hs, func=AF.Identity, scale=rs_[:, 0:1], bias=nb_[:, 0:1]),
                      reads=hr + [("rstd", xi), ("nb", xi)], writes=hr)
                eng = P.pool if eh == 0 else P.dve
                eng(lambda e, hs=hs, eh=eh: e.tensor_tensor(out=hs, in0=hs, in1=lng[:, eh * 512:(eh + 1) * 512], op=ALU.mult), reads=hr + ["lng"], writes=hr)
                eng(lambda e, hs=hs, eh=eh: e.tensor_tensor(out=hs, in0=hs, in1=lnb[:, eh * 512:(eh + 1) * 512], op=ALU.add), reads=hr + ["lnb"], writes=hr)
            P.dma("sp", out_d[tt_ * 128:(tt_ + 1) * 128, :], hb, hbn, reads=[(hbn, 0), (hbn, 1)], writes=["out_d"])

        rm = {"a": 0, "b": 0, "g": 0, "x": 0, "m": 0}
        for c in range(8 if 'M' in STAGES else 0):
            cs = slice(c * 512, (c + 1) * 512)
            P.dma("sp", yscM, ys_d[:, :, cs], "yscM", reads=[("ys_d", 4 * c + i) for i in range(4)], writes=["yscM"])
            for dm in range(8):
                mi = rm["m"] % 2
                rm["m"] += 1
                for br in range(2):
                    slot = rm["a"] % 3
                    rm["a"] += 1
                    b0, b1 = 2 * slot, 2 * slot + 1
                    gi = rm["g"] % 3
                    rm["g"] += 1
                    wup, wupn, act_t, actn, wgt, wgn = ((w_au, "w_au", attnT, "attnT", wga, "wga") if br == 0
                                                        else (w_su, "w_su", None, "yscM", wgs, "wgs"))

                    def fbr(e, dm=dm, c=c, br=br, b0=b0, b1=b1, wup=wup, wgt=wgt):
                        ins = None
                        for cp in range(4):
                            rhs = attnT[:, cp, c * 512:(c + 1) * 512] if br == 0 else yscM[:, cp, :]
                            e.matmul(pb[b0][:, :], lhsT=wup[:, cp, dm * 128:(dm + 1) * 128], rhs=rhs, start=(cp == 0), stop=(cp == 3))
                        for d_ in range(8):
                            ins = e.matmul(pb[b1][:, :], lhsT=wgt[:, d_, dm * 128:(dm + 1) * 128], rhs=xT[:, d_, c * 512:(c + 1) * 512],
                                           start=(d_ == 0), stop=(d_ == 7))
                        return ins
                    P.pe(fbr, reads=[wupn, actn, wgn] + XT_ALL[4 * c:4 * c + 4], writes=[PS(b0), PS(b1)])
                    P.act(lambda e, b1=b1, gi=gi: e.activation(out=sgb[gi], in_=pb[b1][:, :], func=AF.Sigmoid),
                          reads=[PS(b1)], writes=[("sgb", gi)])
                    dstm = mma[mi] if br == 0 else mmb[mi]
                    dstn = ("mma", mi) if br == 0 else ("mmb", mi)
                    P.dve(lambda e, b0=b0, gi=gi, dstm=dstm: e.tensor_tensor(out=dstm, in0=pb[b0][:, :], in1=sgb[gi], op=ALU.mult),
                          reads=[PS(b0), ("sgb", gi)], writes=[dstn])
                P.pool(lambda e, dm=dm, mi=mi: e.tensor_tensor(out=merged[:, dm, :], in0=mma[mi], in1=mmb[mi], op=ALU.add),
                       reads=[("mma", mi), ("mmb", mi)], writes=["merged"])
            for t4 in range(4):
                ln_queue.append((c, t4))
                if len(ln_queue) >= 2:
                    ln_front(*ln_queue[-1])
                    ln_back(*ln_queue[-2])
                else:
                    ln_front(*ln_queue[-1])
        if ln_queue:
            ln_back(*ln_queue[-1])
        fin = [(("hbuf", i), eh) for i in range(2) for eh in range(2)] + ["out_d"] + (["attn_dbg"] if dbg else [])
        P.add("sp", None, reads=fin, writes=fin)
        P.emit()
    return nc


_CACHE = {}


def _consts():
    ident = np.eye(128, dtype=np.float32)
    rr = np.arange(128)
    rowmask = (rr[:, None] // 16 == np.arange(8)[None, :]).astype(np.float32)
    kk = np.arange(128)[:, None]
    qq = np.arange(256)[None, :]
    dist = qq - kk
    valid = (dist >= 0) & (dist <= 128)
    slopes = 2.0 ** (-8.0 * np.arange(1, 9, dtype=np.float64) / 8)
    mask = np.zeros((3, 8, 128, 256), np.float32)
    for g, r in enumerate(DIL):
        for h in range(8):
            mask[g, h] = np.where(valid, np.exp(-slopes[h] * r * np.maximum(dist, 0)), 0.0)
    return ident, rowmask, mask.reshape(24, 128, 256)


def _layout_params(inp):
    lam_re, lam_im, log_dt = inp["lam_re"][0], inp["lam_im"][0], inp["log_dt"][0]
    b_re, b_im, c_re, c_im = inp["b_re"][0], inp["b_im"][0], inp["c_re"][0], inp["c_im"][0]
    r = np.arange(128)
    gl, cc = r // 16, r % 16
    m = {}
    gB = 8 * np.arange(4)[None, :] + gl[:, None]
    m["pB_lr"] = lam_re[gB].reshape(128, 256)
    m["pB_li"] = lam_im[gB].reshape(128, 256)
    m["pB_ldt"] = np.repeat(log_dt[gB][:, :, None], 64, axis=2).reshape(128, 256)
    m["pB_br"] = b_re[gB, :, cc[:, None]].reshape(128, 256)
    m["pB_bi"] = b_im[gB, :, cc[:, None]].reshape(128, 256)
    cre = c_re.reshape(2, 16, 16, 64).transpose(0, 3, 1, 2)
    cim = c_im.reshape(2, 16, 16, 64).transpose(0, 3, 1, 2)
    pc = np.stack([cre, cim], axis=3).reshape(128, 16, 2, 16)
    m["pC_c"] = pc.reshape(128, 512)
    gh, pp = r // 64, r % 64
    gS = 16 * gh[:, None] + np.arange(16)[None, :]
    m["pS_lr"] = lam_re[gS, pp[:, None]]
    m["pS_li"] = lam_im[gS, pp[:, None]]
    m["pS_ldt"] = log_dt[gS]
    m["dsk"] = inp["d_skip"][0].reshape(4, 128).T
    m["bglu"] = inp["b_glu"][0].reshape(4, 128).T
    m["lng"] = np.repeat(inp["ln_g"][0][None, :], 128, axis=0)
    m["lnb"] = np.repeat(inp["ln_b"][0][None, :], 128, axis=0)
    return {k: np.ascontiguousarray(v, dtype=np.float32) for k, v in m.items()}


def kernel(dbg=False, **inp):
    inp = {k: np.asarray(v) for k, v in inp.items()}
    key = bool(dbg)
    if key not in _CACHE:
        _CACHE[key] = build(dbg)
    nc = _CACHE[key]
    ident, rowmask, mask = _consts()
    shared = _layout_params(inp)
    shared.update({
        "w_in": np.ascontiguousarray(inp["w_in"][0]), "w_glu": np.ascontiguousarray(inp["w_glu"][0]),
        "w_attn_up": np.ascontiguousarray(inp["w_attn_up"][0]), "w_ssm_up": np.ascontiguousarray(inp["w_ssm_up"][0]),
        "w_o": np.ascontiguousarray(inp["w_o"][0]), "c_ident": ident, "c_rowmask": rowmask, "c_mask": mask,
    })
    nb = inp["x"].shape[0]
    in_maps = []
    for b in range(nb):
        d = dict(shared)
        d["x"] = np.ascontiguousarray(inp["x"][b])
        in_maps.append(d)
    res = run_bass_kernel_spmd(nc, in_maps, core_ids=list(range(nb)))
    out = np.stack([np.asarray(res.results[b]["out"]) for b in range(nb)], axis=0).astype(np.float32)
    if dbg:
        return out, res
    return out
```
